# Optimizing a Trainium2 kernel written in Bass

```python
import math
import jax, jax.numpy as jnp
from jax import lax
import numpy as np

D_MODEL = 2048
BATCH = 4
SEQ = 2048
DEPTH = 2
DEC_BATCH = 128
DEC_SEQ = 4
PAST_LEN = 16384
PAGE_SIZE = 128

W_A = D_MODEL // 2
SSM_GROUP = 16
G_A = W_A // SSM_GROUP
P_STATE = 64
W_B = D_MODEL // 2
CHUNK_B = 128
G_B = 8
CB = W_B // G_B
DK = 128
DV = 128
W_C = D_MODEL // 2
H_C = W_C // DV
QKV_DIM = 2 * H_C * DK + W_C
CONV_W = 4
CHUNK_C = 64
N_BRANCH = 3
N_IN = 2 * W_A + 3 * W_B + QKV_DIM + 2 * H_C + W_C + N_BRANCH * D_MODEL
_IN_SIZES = (W_A, W_A, W_B, W_B, W_B, QKV_DIM, H_C, H_C, W_C, N_BRANCH * D_MODEL)
ALPHA = (2 * DEPTH) ** 0.25
BETA_INIT = (8 * DEPTH) ** -0.25
LN_EPS = 1e-5
NORM_EPS = 1e-6

kernel_name = 'hybrid_s5_gmlp_gdn_decoder_step'


def _layernorm(x, g, b):
    xf = x.astype(jnp.float32)
    mu = jnp.mean(xf, -1, keepdims=True)
    var = jnp.mean(jnp.square(xf - mu), -1, keepdims=True)
    y = (xf - mu) * lax.rsqrt(var + LN_EPS) * g.astype(jnp.float32) + b.astype(jnp.float32)
    return y.astype(x.dtype)


def _cplx_combine(left, right):
    a1r, a1i, b1r, b1i = left
    a2r, a2i, b2r, b2i = right
    ar = a2r * a1r - a2i * a1i
    ai = a2r * a1i + a2i * a1r
    br = a2r * b1r - a2i * b1i + b2r
    bi = a2r * b1i + a2i * b1r + b2i
    return ar, ai, br, bi


def _s5_branch(xa, h_re, h_im, lam_re, lam_im, log_dt, b_re, b_im, c_re, c_im, d, w_glu, b_glu):
    f32 = jnp.float32
    bn, t = xa.shape[0], xa.shape[1]
    u = xa.astype(f32).reshape(bn, t, G_A, SSM_GROUP)
    lr = lam_re.astype(f32)
    li = lam_im.astype(f32)
    dt = jnp.exp(log_dt.astype(f32))[:, None]
    mag = jnp.exp(lr * dt)
    ab_r = mag * jnp.cos(li * dt)
    ab_i = mag * jnp.sin(li * dt)
    den = lr * lr + li * li
    nr = ab_r - 1.0
    ni = ab_i
    f_r = (nr * lr + ni * li) / den
    f_i = (ni * lr - nr * li) / den
    br = b_re.astype(f32)
    bi = b_im.astype(f32)
    bb_r = f_r[..., None] * br - f_i[..., None] * bi
    bb_i = f_r[..., None] * bi + f_i[..., None] * br
    bu_r = jnp.einsum('gpc,btgc->btgp', bb_r, u)
    bu_i = jnp.einsum('gpc,btgc->btgp', bb_i, u)
    a_r = jnp.broadcast_to(ab_r, bu_r.shape)
    a_i = jnp.broadcast_to(ab_i, bu_i.shape)
    cum_r, cum_i, s_r, s_i = lax.associative_scan(_cplx_combine, (a_r, a_i, bu_r, bu_i), axis=1)
    h0r = h_re.astype(f32)[:, None]
    h0i = h_im.astype(f32)[:, None]
    s_r, s_i = s_r + cum_r * h0r - cum_i * h0i, s_i + cum_r * h0i + cum_i * h0r
    y = jnp.einsum('gcp,btgp->btgc', c_re.astype(f32), s_r) - jnp.einsum('gcp,btgp->btgc', c_im.astype(f32), s_i)
    y = y.reshape(bn, t, W_A) + d.astype(f32) * xa.astype(f32)
    z = jax.nn.gelu(y).astype(xa.dtype)
    z = z * jax.nn.sigmoid(z @ w_glu + b_glu)
    return z, s_r[:, -1].astype(h_re.dtype), s_i[:, -1].astype(h_im.dtype)


def _chunk_gmlp_branch(u, v, ln_v_g, ln_v_b, w_s, b_s):
    bn, t = u.shape[0], u.shape[1]
    vn = _layernorm(v, ln_v_g, ln_v_b)
    pad = (-t) % CHUNK_B
    nc = (t + pad) // CHUNK_B
    vp = jnp.pad(vn, ((0, 0), (0, pad), (0, 0))).reshape(bn, nc, CHUNK_B, G_B, CB)
    ws = w_s * jnp.tril(jnp.ones((CHUNK_B, CHUNK_B), w_s.dtype))
    s = jnp.einsum('gij,bnjgc->bnigc', ws, vp) + b_s.T[None, None, :, :, None]
    s = s.reshape(bn, nc * CHUNK_B, W_B)[:, :t]
    return u * s, vn


def _causal_conv(x, buf, w):
    t = x.shape[1]
    xp = jnp.concatenate([buf.astype(x.dtype), x], axis=1)
    y = sum(xp[:, j:j + t] * w[j] for j in range(CONV_W))
    return y, xp[:, t:]


def _l2norm(x):
    return x * lax.rsqrt(jnp.sum(x * x, -1, keepdims=True) + NORM_EPS)


def _gated_delta_chunked(q, k, v, g, beta, s0):
    bn, t = q.shape[0], q.shape[1]
    pad = (-t) % CHUNK_C
    nc = (t + pad) // CHUNK_C

    def chunks(a):
        a = jnp.pad(a, ((0, 0), (0, pad)) + ((0, 0),) * (a.ndim - 2))
        a = a.reshape((bn, nc, CHUNK_C) + a.shape[2:])
        return jnp.moveaxis(jnp.moveaxis(a, 3, 2), 1, 0)

    q = chunks(q * DK ** -0.5)
    k = chunks(k)
    v = chunks(v)
    g = chunks(g)
    beta = chunks(beta)
    gc = jnp.cumsum(g, axis=-1)
    idx = jnp.arange(CHUNK_C)
    causal = idx[:, None] >= idx[None, :]
    strict = idx[:, None] > idx[None, :]
    diff = gc[..., :, None] - gc[..., None, :]
    decay = jnp.where(causal, jnp.exp(jnp.where(causal, diff, 0.0)), 0.0)
    kb = k * beta[..., None]
    vb = v * beta[..., None]
    lmat = jnp.where(strict, jnp.einsum('nbhid,nbhjd->nbhij', kb, k) * decay, 0.0)
    eye = jnp.eye(CHUNK_C, dtype=jnp.float32)
    rhs = jnp.concatenate([vb, kb * jnp.exp(gc)[..., None]], axis=-1)
    sol = lax.linalg.triangular_solve(lmat + eye, rhs, left_side=True, lower=True, unit_diagonal=True)
    u_c = sol[..., :DV]
    w_c = sol[..., DV:]

    def step(s, inp):
        qi, ki, ui, wi, gi, di = inp
        v_new = ui - jnp.einsum('bhck,bhkv->bhcv', wi, s)
        attn = jnp.einsum('bhik,bhjk->bhij', qi, ki) * di
        o = jnp.einsum('bhck,bhkv->bhcv', qi * jnp.exp(gi)[..., None], s) + jnp.einsum('bhij,bhjv->bhiv', attn, v_new)
        gl = gi[..., -1:]
        s = s * jnp.exp(gl)[..., None] + jnp.einsum('bhck,bhcv->bhkv', ki * jnp.exp(gl - gi)[..., None], v_new)
        return s, o

    s_fin, o = lax.scan(step, s0, (q, k, u_c, w_c, gc, decay))
    o = jnp.moveaxis(jnp.moveaxis(o, 0, 1), 2, 3).reshape(bn, nc * CHUNK_C, H_C, DV)[:, :t]
    return o, s_fin


def _gdn_branch(qkv, a_c, b_c, gate_c, conv_buf, s0, conv_w, a_log, dt_bias, norm_g):
    f32 = jnp.float32
    bn, t = qkv.shape[0], qkv.shape[1]
    y, new_buf = _causal_conv(qkv, conv_buf, conv_w)
    y = jax.nn.silu(y).astype(f32)
    q, k, v = jnp.split(y, [H_C * DK, 2 * H_C * DK], axis=-1)
    q = _l2norm(q.reshape(bn, t, H_C, DK))
    k = _l2norm(k.reshape(bn, t, H_C, DK))
    v = v.reshape(bn, t, H_C, DV)
    g = -jnp.exp(a_log.astype(f32)) * jax.nn.softplus(a_c.astype(f32) + dt_bias.astype(f32))
    beta = jax.nn.sigmoid(b_c.astype(f32))
    o, s_fin = _gated_delta_chunked(q, k, v, g, beta, s0.astype(f32))
    o = o * lax.rsqrt(jnp.mean(o * o, -1, keepdims=True) + NORM_EPS) * norm_g.astype(f32)
    o = o.reshape(bn, t, W_C).astype(qkv.dtype) * jax.nn.silu(gate_c)
    return o, s_fin.astype(s0.dtype), new_buf.astype(conv_buf.dtype)


def _layer(x, h_re, h_im, s0, cbuf, w_in, b_gate, lam_re, lam_im, log_dt, b_re, b_im, c_re, c_im, d,
           w_glu, b_glu, w_pa, ln_v_g, ln_v_b, w_s, b_s, w_pb, conv_w, a_log, dt_bias, norm_g, w_pc,
           w_out, ln_g, ln_b):
    proj = x @ w_in
    split_at = np.cumsum(_IN_SIZES)[:-1].tolist()
    xa, ga, ub, vb, gb, qkv, a_c, b_c, gate_c, zg = jnp.split(proj, split_at, axis=-1)
    ya, h_re_n, h_im_n = _s5_branch(xa, h_re, h_im, lam_re, lam_im, log_dt, b_re, b_im, c_re, c_im, d, w_glu, b_glu)
    pa = (ya * jax.nn.silu(ga)) @ w_pa
    yb, v_rows = _chunk_gmlp_branch(ub, vb, ln_v_g, ln_v_b, w_s, b_s)
    pb = (yb * jax.nn.silu(gb)) @ w_pb
    yc, s_n, cbuf_n = _gdn_branch(qkv, a_c, b_c, gate_c, cbuf, s0, conv_w, a_log, dt_bias, norm_g)
    pc = yc @ w_pc
    g_a, g_b, g_c = jnp.split(jax.nn.sigmoid(zg + b_gate), N_BRANCH, axis=-1)
    out = (g_a * pa + g_b * pb + g_c * pc) @ w_out
    y = _layernorm(ALPHA * x + out, ln_g, ln_b)
    return y, h_re_n, h_im_n, s_n, cbuf_n, v_rows


def setup_inputs(seed: int = 0) -> dict:
    key = jax.random.key(seed)
    ks = list(jax.random.split(key, 40))

    def nrm(shape, scale=1.0):
        return jax.random.normal(ks.pop(), shape, jnp.float32) * scale

    def unif(shape, lo, hi):
        return jax.random.uniform(ks.pop(), shape, jnp.float32, lo, hi)

    dt_c = jnp.exp(unif((DEPTH, H_C), math.log(1e-3), math.log(1e-1)))
    return {
        'x_prompt': nrm((BATCH, SEQ, D_MODEL)),
        'x_sample': nrm((DEC_BATCH, DEC_SEQ, D_MODEL)),
        'state_ssm_re': nrm((DEPTH, DEC_BATCH, G_A, P_STATE), 0.5),
        'state_ssm_im': nrm((DEPTH, DEC_BATCH, G_A, P_STATE), 0.5),
        'state_delta': nrm((DEPTH, DEC_BATCH, H_C, DK, DV), DK ** -0.5),
        'state_conv': nrm((DEPTH, DEC_BATCH, CONV_W - 1, QKV_DIM)),
        'w_in': nrm((DEPTH, D_MODEL, N_IN), D_MODEL ** -0.5),
        'b_gate': nrm((DEPTH, N_BRANCH * D_MODEL), 0.01),
        'lam_re': -0.5 + nrm((DEPTH, G_A, P_STATE), 0.01),
        'lam_im': math.pi * jnp.arange(P_STATE, dtype=jnp.float32) + nrm((DEPTH, G_A, P_STATE), 0.01),
        'log_dt': unif((DEPTH, G_A), math.log(1e-3), math.log(1e-1)),
        'ssm_b_re': nrm((DEPTH, G_A, P_STATE, SSM_GROUP), (2 * SSM_GROUP) ** -0.5),
        'ssm_b_im': nrm((DEPTH, G_A, P_STATE, SSM_GROUP), (2 * SSM_GROUP) ** -0.5),
        'ssm_c_re': nrm((DEPTH, G_A, SSM_GROUP, P_STATE), P_STATE ** -0.5),
        'ssm_c_im': nrm((DEPTH, G_A, SSM_GROUP, P_STATE), P_STATE ** -0.5),
        'ssm_d': nrm((DEPTH, W_A)),
        'w_glu': nrm((DEPTH, W_A, W_A), W_A ** -0.5),
        'b_glu': nrm((DEPTH, W_A), 0.01),
        'w_pa': nrm((DEPTH, W_A, D_MODEL), BETA_INIT * W_A ** -0.5),
        'ln_v_g': 1.0 + nrm((DEPTH, W_B), 0.01),
        'ln_v_b': nrm((DEPTH, W_B), 0.01),
        'w_s': nrm((DEPTH, G_B, CHUNK_B, CHUNK_B), CHUNK_B ** -0.5),
        'b_s': 1.0 + nrm((DEPTH, G_B, CHUNK_B), 0.1),
        'w_pb': nrm((DEPTH, W_B, D_MODEL), BETA_INIT * W_B ** -0.5),
        'conv_w': nrm((DEPTH, CONV_W, QKV_DIM), CONV_W ** -0.5),
        'a_log': jnp.log(unif((DEPTH, H_C), 1.0, 16.0)),
        'dt_bias': dt_c + jnp.log(-jnp.expm1(-dt_c)),
        'gdn_norm_g': 1.0 + nrm((DEPTH, DV), 0.01),
        'w_pc': nrm((DEPTH, W_C, D_MODEL), BETA_INIT * W_C ** -0.5),
        'w_out': nrm((DEPTH, D_MODEL, D_MODEL), BETA_INIT * D_MODEL ** -0.5),
        'ln_g': 1.0 + nrm((DEPTH, D_MODEL), 0.01),
        'ln_b': nrm((DEPTH, D_MODEL), 0.01),
    }


def reference(x_prompt, x_sample, state_ssm_re, state_ssm_im, state_delta, state_conv, w_in, b_gate,
              lam_re, lam_im, log_dt, ssm_b_re, ssm_b_im, ssm_c_re, ssm_c_im, ssm_d, w_glu, b_glu, w_pa,
              ln_v_g, ln_v_b, w_s, b_s, w_pb, conv_w, a_log, dt_bias, gdn_norm_g, w_pc, w_out, ln_g, ln_b):
    layer_params = (w_in, b_gate, lam_re, lam_im, log_dt, ssm_b_re, ssm_b_im, ssm_c_re, ssm_c_im, ssm_d,
                    w_glu, b_glu, w_pa, ln_v_g, ln_v_b, w_s, b_s, w_pb, conv_w, a_log, dt_bias,
                    gdn_norm_g, w_pc, w_out, ln_g, ln_b)
    bp = x_prompt.shape[0]
    h0 = jnp.zeros((bp, G_A, P_STATE), state_ssm_re.dtype)
    s0 = jnp.zeros((bp, H_C, DK, DV), state_delta.dtype)
    c0 = jnp.zeros((bp, CONV_W - 1, QKV_DIM), state_conv.dtype)
    y_prompt, y_sample = x_prompt, x_sample
    p_re, p_im, p_s, p_c = [], [], [], []
    q_re, q_im, q_s, q_c, q_v = [], [], [], [], []
    for l in range(DEPTH):
        lp = tuple(p[l] for p in layer_params)
        y_prompt, hr, hi, sd, cb, _ = _layer(y_prompt, h0, h0, s0, c0, *lp)
        p_re.append(hr)
        p_im.append(hi)
        p_s.append(sd)
        p_c.append(cb)
        y_sample, hr, hi, sd, cb, vr = _layer(y_sample, state_ssm_re[l], state_ssm_im[l], state_delta[l],
                                              state_conv[l], *lp)
        q_re.append(hr)
        q_im.append(hi)
        q_s.append(sd)
        q_c.append(cb)
        q_v.append(vr)
    return (y_prompt, y_sample, jnp.stack(p_re), jnp.stack(p_im), jnp.stack(p_s), jnp.stack(p_c),
            jnp.stack(q_re), jnp.stack(q_im), jnp.stack(q_s), jnp.stack(q_c), jnp.stack(q_v))
```

```python
import contextlib
import math
import numpy as np
import concourse.bass as bass
import concourse.mybir as mybir
from concourse.bass_utils import run_bass_kernel_spmd

F32 = mybir.dt.float32
F32R = mybir.dt.float32r
BF16 = mybir.dt.bfloat16
AF = mybir.ActivationFunctionType
ALU = mybir.AluOpType

D = 2048
SEQ = 2048
NL = 2
NIN = 15376
QKV = 3072
HC = 8
BLK = 512
NBLK = SEQ // BLK
NSEQ = 16
NS = NSEQ * 4
NT = BLK + NS
O_XA, O_GA, O_UB, O_VB, O_GB, O_QKV, O_AC, O_GC, O_ZG = 0, 1024, 2048, 3072, 4096, 5120, 8192, 8208, 9232
ALPHA = (2 * NL) ** 0.25
MAGIC = 12582912.0
TWO_PI = 2.0 * math.pi

ENGS = ("pe", "act", "dve", "pool", "sp")
EPOCH = 12000
N_DMA_SEMS = 40
DMA_DESC_LIMIT = 6144
WRITE_KEYS = ("out", "ap", "accum_out")


class Res:
    __slots__ = ("lw", "rd", "excl")

    def __init__(self, excl=False):
        self.lw = None
        self.rd = []
        self.excl = excl


class V:
    __slots__ = ("ap", "res")

    def __init__(self, ap, res):
        self.ap = ap
        self.res = tuple(res)

    def __getitem__(self, k):
        return V(self.ap[k], self.res)

    def r(self):
        return V(self.ap.bitcast(F32R), self.res)

    def f(self):
        return V(self.ap.bitcast(F32), self.res)

    def bc(self, shape):
        return V(self.ap.to_broadcast(list(shape)), self.res)

    def re(self, pat_, **kw):
        return V(self.ap.rearrange(pat_, **kw), self.res)

    def un(self, ax):
        return V(self.ap.unsqueeze(ax), self.res)


class Op:
    __slots__ = ("eng", "meth", "kw", "deps", "is_dma", "sig", "sigidx", "dsem", "dval")


class Prog:
    def __init__(self, nc):
        self.nc = nc
        self.ops = []
        self.per_eng = {e: [] for e in ENGS}
        self.dma_rr = 0
        self.dma_last = [None] * N_DMA_SEMS
        self.dma_cnt = [0] * N_DMA_SEMS
        self.out_dmas = []
        self.inflight = {e: [] for e in ENGS}

    def _add(self, eng, meth, kw, is_dma):
        op = Op()
        op.eng, op.meth, op.kw, op.is_dma = eng, meth, kw, is_dma
        op.deps = set()
        op.sig = False
        op.sigidx = None
        op.dsem = None
        op.dval = None
        reads, writes = [], []
        for k, v in kw.items():
            if isinstance(v, V):
                for r_ in v.res:
                    (writes if (k in WRITE_KEYS or r_.excl) else reads).append(r_)
        for r in reads:
            if r.lw is not None:
                op.deps.add(r.lw)
        for w in writes:
            if w.lw is not None:
                op.deps.add(w.lw)
            op.deps.update(w.rd)
        for r in reads:
            r.rd.append(op)
        for w in writes:
            w.lw = op
            w.rd = []
        if is_dma:
            s = self.dma_rr % N_DMA_SEMS
            self.dma_rr += 1
            if self.dma_last[s] is not None:
                op.deps.add(self.dma_last[s])
            self.dma_last[s] = op
            self.dma_cnt[s] += 16
            op.dsem = s
            op.dval = self.dma_cnt[s]
        op.deps.discard(op)
        self.ops.append(op)
        self.per_eng[eng].append(op)
        return op

    def I(self, eng, meth, **kw):
        return self._add(eng, meth, kw, False)

    @staticmethod
    def _ndesc(a):
        ap = a.ap if isinstance(a, V) else a
        dims = [list(d) for d in ap.ap]
        n = 1
        for (st_, sz) in dims:
            n *= sz
        if dims and dims[-1][0] == 1:
            n //= dims[-1][1]
        return max(1, n)

    def dma(self, eng, is_out=False, **kw):
        nd = max(self._ndesc(kw["out"]), self._ndesc(kw["in_"]))
        fl = self.inflight["sp"]
        extra = []
        while fl and sum(x[1] for x in fl) + nd > DMA_DESC_LIMIT:
            extra.append(fl.pop(0)[0])
        o = self._add(eng, "dma_start", kw, True)
        o.deps.update(extra)
        fl.append((o, nd))
        if is_out:
            self.out_dmas.append(o)
        return o

    def emit(self):
        nc = self.nc
        fin = Op()
        fin.eng, fin.meth, fin.kw, fin.is_dma = "sp", None, {}, False
        fin.deps = set(self.out_dmas)
        fin.sig = False
        fin.sigidx = None
        self.ops.append(fin)
        self.per_eng["sp"].append(fin)
        for op in self.ops:
            for d in op.deps:
                if not d.is_dma:
                    if d.eng == "pe" and op.eng == "pe" and not op.is_dma:
                        continue
                    d.sig = True
        nsig = {}
        for e in ENGS:
            c = 0
            for op in self.per_eng[e]:
                if op.sig and not op.is_dma:
                    c += 1
                    op.sigidx = c
            nsig[e] = c
        with contextlib.ExitStack() as st:
            esems = {}
            for e in ENGS:
                n_ep = max(1, (nsig[e] + EPOCH - 1) // EPOCH)
                esems[e] = [st.enter_context(nc.semaphore(f"s_{e}_{i}")) for i in range(n_ep)]
            dsems = [st.enter_context(nc.semaphore(f"s_dma_{i}")) for i in range(N_DMA_SEMS)]
            block = st.enter_context(nc.Block())

            def run(e, eng):
                seen = {}
                for op in self.per_eng[e]:
                    waits = {}
                    for d in op.deps:
                        if d.is_dma:
                            key, val, sem = ("d", d.dsem), d.dval, dsems[d.dsem]
                        else:
                            if d.eng == "pe" and e == "pe" and not op.is_dma:
                                continue
                            ep = (d.sigidx - 1) // EPOCH
                            key, val, sem = (d.eng, ep), d.sigidx - ep * EPOCH, esems[d.eng][ep]
                        if seen.get(key, 0) >= val:
                            continue
                        if key not in waits or waits[key][1] < val:
                            waits[key] = (sem, val)
                    for key, (sem, val) in waits.items():
                        eng.wait_ge(sem, val)
                        seen[key] = val
                    if op.meth is None:
                        continue
                    args = {k: (v.ap if isinstance(v, V) else v) for k, v in op.kw.items()}
                    inst = getattr(eng, op.meth)(**args)
                    if op.is_dma:
                        inst.then_inc(dsems[op.dsem], 16)
                    elif op.sig:
                        inst.then_inc(esems[e][(op.sigidx - 1) // EPOCH], 1)

            @block.tensor
            def _(eng):
                run("pe", eng)

            @block.scalar
            def _(eng):
                run("act", eng)

            @block.vector
            def _(eng):
                run("dve", eng)

            @block.gpsimd
            def _(eng):
                run("pool", eng)

            @block.sync
            def _(eng):
                run("sp", eng)


class Arena:
    def __init__(self, tensor, ncols):
        self.t = tensor
        self.n = ncols
        self.ptr = 0
        self.live = []
        self.retired = []

    def reset(self):
        self.retired = [x for x in self.retired] + self.live
        self.retired = self.retired[-400:]
        self.live = []
        self.ptr = 0

    def mark(self):
        return (self.ptr, len(self.live))

    def release(self, mk):
        ptr, nl = mk
        self.retired = (self.retired + self.live[nl:])[-400:]
        self.live = self.live[:nl]
        self.ptr = ptr

    def alloc(self, n, parts=128):
        a, b = self.ptr, self.ptr + n
        assert b <= self.n, f"arena overflow {b} > {self.n}"
        self.ptr = b
        r = Res()
        for (s, e, rr) in self.retired:
            if s < b and a < e:
                if rr.lw is not None:
                    r.rd.append(rr.lw)
                r.rd.extend(rr.rd)
        self.live.append((a, b, r))
        return V(self.t[0:parts, a:b], [r])


def build_program():
    nc = bass.Bass("TRN2", target_bir_lowering=False)
    P = Prog(nc)

    def din(name, shape):
        return nc.dram_tensor(name, list(shape), F32, kind="ExternalInput").ap()

    def dout(name, shape):
        return nc.dram_tensor(name, list(shape), F32, kind="ExternalOutput").ap()

    xp = din("xp", [SEQ, D])
    xs = din("xs", [NS, D])
    st_re = din("st_re", [NL, NSEQ, 4096])
    st_im = din("st_im", [NL, NSEQ, 4096])
    st_dl = din("st_dl", [NL, NSEQ, HC, 128, 128])
    st_cv = din("st_cv", [NL, NS - NSEQ, QKV])
    w_in = din("w_in", [NL, D, NIN])
    b_gate = din("b_gate", [NL, 48, 128])
    lam_re = din("lam_re", [NL, 32, 128])
    lam_im = din("lam_im", [NL, 32, 128])
    log_dt = din("log_dt", [NL, 32, 2])
    b_re = din("b_re", [NL, 4096, 16])
    b_im = din("b_im", [NL, 4096, 16])
    c_re = din("c_re", [NL, 1024, 64])
    c_im = din("c_im", [NL, 1024, 64])
    ssm_d = din("ssm_d", [NL, 8, 128])
    w_glu = din("w_glu", [NL, 1024, 1024])
    b_glu = din("b_glu", [NL, 8, 128])
    w_pa = din("w_pa", [NL, 1024, D])
    ln_v_g = din("ln_v_g", [NL, 1024])
    ln_v_b = din("ln_v_b", [NL, 1024])
    w_s = din("w_s", [NL, 8, 128, 128])
    b_s = din("b_s", [NL, 1024])
    w_pb = din("w_pb", [NL, 1024, D])
    conv_w = din("conv_w", [NL, 96, 128])
    a_log = din("a_log", [NL, 8])
    dt_bias = din("dt_bias", [NL, 8])
    norm_g = din("norm_g", [NL, 1, 128])
    w_pc = din("w_pc", [NL, 1024, D])
    w_out = din("w_out", [NL, D, D])
    ln_g = din("ln_g", [NL, 16, 128])
    ln_b = din("ln_b", [NL, 16, 128])

    y_p = dout("y_p", [SEQ, D])
    y_s = dout("y_s", [NS, D])
    sre_p = dout("sre_p", [NL, 32, 128])
    sim_p = dout("sim_p", [NL, 32, 128])
    dl_p = dout("dl_p", [NL, HC, 128, 128])
    cv_p = dout("cv_p", [NL, 3, QKV])
    sre_s = dout("sre_s", [NL, NSEQ, 4096])
    sim_s = dout("sim_s", [NL, NSEQ, 4096])
    dl_s = dout("dl_s", [NL, NSEQ, HC, 128, 128])
    cv_s = dout("cv_s", [NL, NSEQ, 3, QKV])
    gv_s = dout("gv_s", [NL, NS, 1024])

    st = contextlib.ExitStack()
    with st:
        def sbt(name, shape, dt=F32):
            return st.enter_context(nc.sbuf_tensor(name, list(shape), dt))

        def pst(name, shape):
            return st.enter_context(nc.psum_tensor(name, list(shape), F32))

        xT_t = sbt("xT", [128, 16, NT])
        mg_t = sbt("mg", [128, 16, NT])
        NWB = 3
        WBW = 2048
        wb_t = sbt("wbuf", [128, NWB, WBW], F32R)
        NWB16 = 2
        wb16_t = sbt("wbuf16", [128, NWB16, WBW], BF16)
        ACOLS = 12200
        ar_t = sbt("arena", [128, ACOLS])
        ar = Arena(ar_t, ACOLS)
        A16 = 9728
        ar16_t = sbt("arena16", [128, A16], BF16)
        ar16 = Arena(ar16_t, A16)
        cst_t = sbt("consts", [128, 7, 128])
        sm_t = sbt("small", [128, 640])
        gS_t = sbt("gdnS", [128, NL * HC, 128])
        s5p_t = sbt("s5p", [128, 4096])
        ws_t = sbt("wsT", [128, 8, 128], BF16)
        wss_t = sbt("wsTs", [64, 8, 64], BF16)
        misc_t = sbt("misc", [128, 1024])

        xT = [V(xT_t[:, k, :], [Res()]) for k in range(16)]
        mg = [V(mg_t[:, k, :], [Res()]) for k in range(16)]
        wbufs = [V(wb_t[:, i, :], [Res()]) for i in range(NWB)]
        wbufs16 = [V(wb16_t[:, i, :], [Res()]) for i in range(NWB16)]
        xTr = [x.r() for x in xT]
        ones = V(cst_t[:, 0, :], [Res()])
        ident = V(cst_t[:, 1, :], [Res()])
        mask_ui = V(cst_t[:, 2, :], [Res()])
        negmask_su = V(cst_t[:, 3, :], [Res()])
        krow = V(cst_t[:, 4, :], [Res()])
        bm64 = V(cst_t[0:64, 5, 0:64], [Res()])
        rep4 = V(cst_t[0:4, 6, 0:64], [Res()])
        negones = V(cst_t[:, 6, 64:128], [Res()])
        pmask = V(cst_t[:, 5, 64:68], [Res()])

        smo = [0]

        def sm_alloc(n):
            a = smo[0]
            smo[0] += n
            assert smo[0] <= 640
            return V(sm_t[:, a:a + n], [Res()])

        bgate_c = [sm_alloc(48) for _ in range(NL)]
        ssmd_c = [sm_alloc(8) for _ in range(NL)]
        bglu_c = [sm_alloc(8) for _ in range(NL)]
        lng_c = [sm_alloc(16) for _ in range(NL)]
        lnb_c = [sm_alloc(16) for _ in range(NL)]
        convw_c = [sm_alloc(96) for _ in range(NL)]
        normg_c = [sm_alloc(1) for _ in range(NL)]
        s5st_re = [sm_alloc(32) for _ in range(NL)]
        s5st_im = [sm_alloc(32) for _ in range(NL)]
        ctail = [[V(misc_t[:, (l * 24 + c) * 3:(l * 24 + c) * 3 + 3], [Res()]) for c in range(24)] for l in range(NL)]
        gS = [[V(gS_t[:, l * HC + h, :], [Res()]) for h in range(HC)] for l in range(NL)]
        mo = [NL * 24 * 3]

        def misc_alloc(n, parts=128):
            a = mo[0]
            mo[0] += n
            assert mo[0] <= 1024
            return V(misc_t[0:parts, a:a + n], [Res()])

        negea = [misc_alloc(8) for _ in range(NL)]
        dtb = [misc_alloc(8) for _ in range(NL)]

        banks = [pst(f"ps{i}", [128, 512]) for i in range(8)]
        bres = [Res(excl=True) for _ in range(8)]
        bankV = [V(banks[i][:, :], [bres[i]]) for i in range(8)]
        quarter = [[V(banks[i][:, q * 128:(q + 1) * 128], [bres[i]]) for q in range(4)] for i in range(8)]
        slot64 = [[V(banks[i][:, q * 64:(q + 1) * 64], [bres[i]]) for q in range(8)] for i in range(8)]

        wbi = [0]

        def wload(src2d, kc, cols):
            assert kc * cols <= WBW
            b = wbufs[wbi[0] % NWB]
            wbi[0] += 1
            dst = b[:, 0:kc * cols].re("p (k c) -> p k c", c=cols)
            P.dma("pool", out=dst, in_=src2d.rearrange("(k p) c -> p k c", p=128))
            return dst

        wbi16 = [0]

        def wload16(src2d, kc, cols):
            assert kc * cols <= WBW
            b = wbufs16[wbi16[0] % NWB16]
            wbi16[0] += 1
            dst = b[:, 0:kc * cols].re("p (k c) -> p k c", c=cols)
            P.dma("pool", out=dst, in_=src2d.rearrange("(k p) c -> p k c", p=128))
            return dst

        def act(out, in_, func, bias=0.0, scale=1.0, eng="act"):
            if isinstance(bias, float) and bias == 0.0:
                P.I("act", "activation", out=out, in_=in_, func=func, scale=scale)
            else:
                P.I("act", "activation", out=out, in_=in_, func=func, bias=bias, scale=scale)

        def tt(out, in0, in1, op, eng="dve"):
            P.I(eng, "tensor_tensor", out=out, in0=in0, in1=in1, op=op)

        def ts(out, in0, s1, op0, s2=None, op1=None, eng="dve"):
            if op1 is None:
                P.I(eng, "tensor_scalar", out=out, in0=in0, scalar1=s1, scalar2=None, op0=op0)
            else:
                P.I(eng, "tensor_scalar", out=out, in0=in0, scalar1=s1, scalar2=s2, op0=op0, op1=op1)

        def stt(out, in0, scalar, in1, op0, op1, eng="dve"):
            P.I(eng, "scalar_tensor_tensor", out=out, in0=in0, scalar=scalar, in1=in1, op0=op0, op1=op1)

        def cp(out, in_, eng="act"):
            if eng == "act":
                P.I("act", "copy", out=out, in_=in_)
            else:
                P.I(eng, "tensor_copy", out=out, in_=in_)

        def mm(out, lhsT, rhs, start=True, stop=True, tp=None):
            if tp is None:
                P.I("pe", "matmul", out=out, lhsT=lhsT, rhs=rhs, start=start, stop=stop)
            else:
                P.I("pe", "matmul", out=out, lhsT=lhsT, rhs=rhs, start=start, stop=stop, tile_position=tp)

        def tr(out, in_, n):
            P.I("pe", "transpose", out=out, in_=in_, identity=ident[0:n, 0:n])

        rr = {"main": 0, "s64": 0, "aux": 0}

        def ps_main():
            b = (0, 1)[rr["main"] % 2]
            rr["main"] += 1
            return bankV[b]

        def ps_s64():
            q = rr["s64"] % 8
            rr["s64"] += 1
            return slot64[2][q]

        def ps_aux():
            b = (3, 4)[rr["aux"] % 2]
            rr["aux"] += 1
            return bankV[b]

        def ps_for(gi, n):
            return ps_main()[:, 0:n] if gi == 0 else ps_s64()[:, 0:n]

        def proj_fm(wsrc, kc, ncols, rhs, groups, consume, getps=ps_for, loader=None, wview=None):
            cg = WBW // (kc * 128)
            cg = min(cg, ncols // 128)
            for c0 in range(0, ncols, cg * 128):
                w = (loader or wload)(wsrc[:, c0:c0 + cg * 128], kc, cg * 128)
                if wview is not None:
                    w = wview(w)
                for mi in range(cg):
                    m = c0 // 128 + mi
                    for gi, (col0, n) in enumerate(groups):
                        ps = getps(gi, n)
                        for k in range(kc):
                            mm(ps, w[:, k, mi * 128:(mi + 1) * 128], rhs[k][:, col0:col0 + n], start=(k == 0), stop=(k == kc - 1))
                        consume(m, gi, ps, col0, n)

        def load_cols(dst, src2d, n):
            stg = ar.alloc(128, parts=n)
            P.dma("sp", out=stg, in_=src2d)
            ps = quarter[7][3][:, 0:n]
            tr(ps, stg, n)
            cp(dst, ps, eng="dve")

        P.I("pool", "memset", ap=ones, constant=1.0)
        P.I("pool", "memset", ap=negones, constant=-1.0)
        P.I("pool", "affine_select", out=ident, in_=ones, pattern=[[-1, 128]], compare_op=ALU.is_equal, fill=0.0, base=0, channel_multiplier=1)
        P.I("pool", "affine_select", out=mask_ui, in_=ones, pattern=[[1, 128]], compare_op=ALU.is_ge, fill=0.0, base=0, channel_multiplier=-1)
        P.I("pool", "affine_select", out=negmask_su[:, 0:64], in_=negones, pattern=[[1, 64]], compare_op=ALU.is_ge, fill=0.0, base=-1, channel_multiplier=-1)
        P.I("pool", "affine_select", out=negmask_su[:, 64:128], in_=negones, pattern=[[1, 64]], compare_op=ALU.is_ge, fill=0.0, base=63, channel_multiplier=-1)
        P.I("pool", "iota", out=krow, pattern=[[1, 128]], base=1, channel_multiplier=0, allow_small_or_imprecise_dtypes=True)
        for q_ in range(4):
            P.I("pool", "affine_select", out=pmask[:, q_:q_ + 1], in_=ones[:, 0:1], pattern=[[0, 1]], compare_op=ALU.is_ge, fill=0.0, base=-32 * q_, channel_multiplier=1)
            P.I("pool", "affine_select", out=pmask[:, q_:q_ + 1], in_=pmask[:, q_:q_ + 1], pattern=[[0, 1]], compare_op=ALU.is_ge, fill=0.0, base=32 * q_ + 31, channel_multiplier=-1)
        cp(bm64, mask_ui[0:64, 0:64], eng="pool")
        for a in range(1, 16):
            P.I("pool", "affine_select", out=bm64[:, 4 * a:4 * a + 4], in_=bm64[:, 4 * a:4 * a + 4], pattern=[[0, 4]], compare_op=ALU.is_ge, fill=0.0, base=-4 * a, channel_multiplier=1)
        cp(rep4.re("p (b j) -> p b j", j=4), ident[0:4, 0:4].un(1).bc([4, 16, 4]), eng="pool")
        for l in range(NL):
            for k in range(24):
                P.I("pool", "memset", ap=ctail[l][k], constant=0.0)
            for h in range(HC):
                P.I("pool", "memset", ap=gS[l][h], constant=0.0)
            P.I("pool", "memset", ap=s5st_re[l], constant=0.0)
            P.I("pool", "memset", ap=s5st_im[l], constant=0.0)

        ar.reset()
        for l in range(NL):
            load_cols(bgate_c[l], b_gate[l], 48)
            load_cols(ssmd_c[l], ssm_d[l], 8)
            load_cols(bglu_c[l], b_glu[l], 8)
            load_cols(lng_c[l], ln_g[l], 16)
            load_cols(lnb_c[l], ln_b[l], 16)
            load_cols(convw_c[l], conv_w[l], 96)
            load_cols(normg_c[l], norm_g[l], 1)
            P.dma("sp", out=negea[l], in_=a_log[l].partition_broadcast(128))
            act(negea[l], negea[l], AF.Exp)
            ts(negea[l], negea[l], -1.0, ALU.mult)
            P.dma("sp", out=dtb[l], in_=dt_bias[l].partition_broadcast(128))

        def load_x(blk, groups):
            ar.reset()
            tiles = [(xp[blk * BLK + i * 128: blk * BLK + (i + 1) * 128, :], i * 128, 128) for i in range(4)]
            if blk == 0:
                tiles.append((xs, BLK, NS))
            stg = [ar.alloc(D), ar.alloc(D)]
            for ti, (src, col0, n) in enumerate(tiles):
                s = stg[ti % 2]
                P.dma("sp", out=s[0:n, :], in_=src)
                for k4 in range(4):
                    ps = ps_main()
                    for j in range(4):
                        k = k4 * 4 + j
                        tr(ps[:, j * 128:j * 128 + n], s[0:n, k * 128:(k + 1) * 128], n)
                    for j in range(4):
                        k = k4 * 4 + j
                        cp(xT[k][:, col0:col0 + n].r(), ps[:, j * 128:j * 128 + n], eng="dve")

        def s5_prep(l):
            o = [0]

            def pal(n):
                a = o[0]
                o[0] += n
                assert o[0] <= 4096
                return V(s5p_t[:, a:a + n], [Res()])

            BTre = pal(1024).re("p (c s) -> p c s", s=128)
            BTim = pal(1024).re("p (c s) -> p c s", s=128)
            CTre = pal(1024).re("p (q c) -> p q c", c=32)
            CTim = pal(1024).re("p (q c) -> p q c", c=32)
            mag, th, abr, abi = [ar.alloc(32) for _ in range(4)]
            mk_ = ar.mark()
            lr, li, dt_, fr, fi, t0, t1, t2 = [ar.alloc(32) for _ in range(8)]
            load_cols(lr, lam_re[l], 32)
            load_cols(li, lam_im[l], 32)
            ldt = ar.alloc(2, parts=32)
            P.dma("sp", out=ldt, in_=log_dt[l])
            ldtb = ar.alloc(128, parts=32)
            cp(ldtb.re("p (g s) -> p g s", s=64), ldt.un(2).bc([32, 2, 64]), eng="dve")
            psq = quarter[7][2][:, 0:32]
            tr(psq, ldtb, 32)
            act(dt_, psq, AF.Exp)
            tt(t0, lr, dt_, ALU.mult)
            act(mag, t0, AF.Exp)
            tt(th, li, dt_, ALU.mult)

            def rsin(out, ang, shift):
                a2 = ar.alloc(ang.ap.shape[1])
                k = ar.alloc(ang.ap.shape[1])
                if shift != 0.0:
                    ts(a2, ang, shift, ALU.add)
                else:
                    cp(a2, ang, eng="dve")
                ts(k, a2, 1.0 / TWO_PI, ALU.mult, MAGIC, ALU.add)
                ts(k, k, MAGIC, ALU.subtract, TWO_PI, ALU.mult)
                tt(a2, a2, k, ALU.subtract)
                act(out, a2, AF.Sin)

            rsin(t1, th, 0.0)
            rsin(t2, th, math.pi / 2)
            tt(abr, mag, t2, ALU.mult)
            tt(abi, mag, t1, ALU.mult)
            den, nr, u0, u1 = [ar.alloc(32) for _ in range(4)]
            tt(den, lr, lr, ALU.mult)
            tt(u0, li, li, ALU.mult)
            tt(den, den, u0, ALU.add)
            P.I("dve", "reciprocal", out=den, in_=den)
            ts(nr, abr, -1.0, ALU.add)
            tt(u0, nr, lr, ALU.mult)
            tt(u1, abi, li, ALU.mult)
            tt(u0, u0, u1, ALU.add)
            tt(fr, u0, den, ALU.mult)
            tt(u0, abi, lr, ALU.mult)
            tt(u1, nr, li, ALU.mult)
            tt(u0, u0, u1, ALU.subtract)
            tt(fi, u0, den, ALU.mult)
            import os
            ks5 = float(os.environ.get("KS5", "9"))
            if ks5 <= 1:
                return None
            Br = ar.alloc(512).re("p (q c) -> p q c", c=16)
            Bi = ar.alloc(512).re("p (q c) -> p q c", c=16)
            for i4 in range(4):
                qs = slice(8 * i4, 8 * i4 + 8)
                P.dma("sp", out=Br[:, qs, :], in_=b_re[l].rearrange("(q s) c -> s q c", s=128)[:, qs, :])
                P.dma("sp", out=Bi[:, qs, :], in_=b_im[l].rearrange("(q s) c -> s q c", s=128)[:, qs, :])
            if ks5 <= 1.1:
                return None
            frb = fr.un(2).bc([128, 32, 16])
            fib = fi.un(2).bc([128, 32, 16])
            w0 = ar.alloc(512).re("p (q c) -> p q c", c=16)
            w1 = ar.alloc(512).re("p (q c) -> p q c", c=16)
            INr = ar.alloc(1024).re("p (q c) -> p q c", c=32)
            INi = ar.alloc(1024).re("p (q c) -> p q c", c=32)
            P.I("dve", "memset", ap=INr, constant=0.0)
            P.I("dve", "memset", ap=INi, constant=0.0)
            if ks5 <= 1.2:
                return None
            tt(w0, Br, frb, ALU.mult)
            tt(w1, Bi, fib, ALU.mult)
            tt(INr[0:64, :, 0:16], w0[0:64], w1[0:64], ALU.subtract)
            tt(INr[64:128, :, 16:32], w0[64:128], w1[64:128], ALU.subtract)
            tt(w0, Bi, frb, ALU.mult)
            tt(w1, Br, fib, ALU.mult)
            tt(INi[0:64, :, 0:16], w0[0:64], w1[0:64], ALU.add)
            tt(INi[64:128, :, 16:32], w0[64:128], w1[64:128], ALU.add)
            if ks5 <= 1.5:
                return None
            kc_ = int(os.environ.get("KC", "8"))
            kv_ = os.environ.get("KV", "")
            for c in range(kc_):
                for (IN, BT, qd) in ((INr, BTre, 0), (INi, BTim, 1)):
                    ps = quarter[7][qd]
                    if kv_ == "notr_id":
                        tr(ps, ones, 128)
                        continue
                    tr(ps, IN[:, 4 * c:4 * c + 4, :].re("p q c -> p (q c)"), 128)
                    if kv_ == "notr":
                        continue
                    cp(BT[:, c, :], ps, eng=("dve" if kv_ == "dve" else ("act" if qd == 0 else "dve")))
            if ks5 <= 2:
                return None
            for (csrc, CT, sign) in ((c_re, CTre, 1.0), (c_im, CTim, -1.0)):
                Cl = ar.alloc(1024).re("p (c d s) -> p c d s", d=2, s=64)
                P.dma("sp", out=Cl[:, :, 0, :], in_=csrc[l].rearrange("(c r) s -> r c s", r=128))
                P.dma("sp", out=Cl[:, :, 1, :], in_=csrc[l].rearrange("(c r) s -> r c s", r=128))
                P.I("dve", "memset", ap=CT, constant=0.0)
                for c in range(8):
                    ps = quarter[7][c % 2]
                    tr(ps, Cl[:, c, :, :].re("p d s -> p (d s)"), 128)
                    psv = ps.re("p (q g c) -> p q g c", g=2, c=16)
                    if sign > 0:
                        cp(CT[0:64, 4 * c:4 * c + 4, 0:16], psv[0:64, :, 0, :], eng="act")
                        cp(CT[64:128, 4 * c:4 * c + 4, 16:32], psv[64:128, :, 1, :], eng="dve")
                    else:
                        ts(CT[0:64, 4 * c:4 * c + 4, 0:16], psv[0:64, :, 0, :], -1.0, ALU.mult)
                        ts(CT[64:128, 4 * c:4 * c + 4, 16:32], psv[64:128, :, 1, :], -1.0, ALU.mult)
            ar.release(mk_)
            return dict(BTre=BTre, BTim=BTim, CTre=CTre, CTim=CTim, mag=mag, th=th, abr=abr, abi=abi)

        def merge_branch(l, br, wp, yT, groups):
            def getps(gi, n):
                return ps_main()[:, 0:n] if gi == 0 else ps_s64()[:, 0:n]

            for m in range(16):
                if m % 2 == 0:
                    wP2 = wload16(wp[:, m * 128:(m + 2) * 128], 8, 256)
                wP = wP2[:, :, (m % 2) * 128:(m % 2) * 128 + 128]
                zc = O_ZG + br * D + m * 128
                wZ = wload(w_in[l][:, zc:zc + 128], 16, 128)
                for gi, (col0, n) in enumerate(groups):
                    pp = getps(gi, n)
                    for k in range(8):
                        mm(pp, wP[:, k, :], yT[k][:, col0:col0 + n], start=(k == 0), stop=(k == 7))
                    pz = (ps_aux()[:, 0:n] if gi == 0 else ps_s64()[:, 0:n])
                    for k in range(16):
                        mm(pz, wZ[:, k, :], xT[k][:, col0:col0 + n].r(), start=(k == 0), stop=(k == 15))
                    g = mtmp[gi % 2][:, 0:n]
                    act(g, pz, AF.Sigmoid, bias=bgate_c[l][:, br * 16 + m:br * 16 + m + 1])
                    if br == 0:
                        tt(mg[m][:, col0:col0 + n].r(), g, pp, ALU.mult)
                    else:
                        tt(g, g, pp, ALU.mult)
                        tt(mg[m][:, col0:col0 + n].r(), mg[m][:, col0:col0 + n].f(), g, ALU.add)

        def phase_A(l, blk, groups):
            ar.reset()
            ar16.reset()
            global mtmp
            import os
            ksub = int(os.environ.get("KSUB", "99"))
            s5 = s5_prep(l)
            if ksub <= 0:
                return
            uT = [ar.alloc(NT) for _ in range(8)]
            ya = [ar16.alloc(NT) for _ in range(8)]

            def cons_u(m, gi, ps, c0, n):
                cp(uT[m][:, c0:c0 + n], ps, eng="act")

            proj_fm(w_in[l][:, O_XA:O_XA + 1024], 16, 1024, xTr, groups, cons_u)
            if ksub <= 1:
                return
            cosT, sinT, ang, rk, t1, t2, t3, s_re, s_im, t4 = [ar.alloc(BLK) for _ in range(10)]
            ytmp_s = ar.alloc(NS)
            um = [ar.alloc(NT), ar.alloc(NT)]
            has_s = len(groups) > 1
            if has_s:
                sst_re = [ar.alloc(128, parts=NSEQ)] * 2
                sst_im = [ar.alloc(128, parts=NSEQ)] * 2
                so_re = [ar.alloc(128, parts=NSEQ)] * 2
                so_im = [ar.alloc(128, parts=NSEQ)] * 2
                ss_re = [ar.alloc(NSEQ), ar.alloc(NS)]
                ss_im = [ar.alloc(NSEQ), ar.alloc(NS)]
                stm = [ar.alloc(NSEQ) for _ in range(2)]
            v3 = lambda v: v.re("p (n k) -> p n k", k=128)
            for c in range(8 if ksub > 2 else 1):
                ypb = bankV[6]
                for ql in range(4):
                    q = 4 * c + ql
                    rs = slice(32 * ql, 32 * ql + 32)
                    ts(ang, krow512, s5["th"][:, q:q + 1], ALU.mult)
                    ts(rk, ang, 1.0 / TWO_PI, ALU.mult, MAGIC, ALU.add)
                    ts(rk, rk, MAGIC, ALU.subtract, TWO_PI, ALU.mult)
                    tt(ang, ang, rk, ALU.subtract)
                    act(sinT, ang, AF.Sin)
                    act(rk, ang, AF.Abs)
                    act(cosT, rk, AF.Sin, bias=halfpi, scale=-1.0)
                    pbr, pbi = bankV[3], bankV[4]
                    umq = um[q % 2]
                    ncol = NT if has_s else BLK
                    P.I("act", "activation", out=umq[:, 0:ncol], in_=uT[c][:, 0:ncol], func=AF.Copy, scale=pmask[:, ql:ql + 1])
                    mm(pbr, s5["BTre"][:, c, :], umq[:, 0:BLK])
                    mm(pbi, s5["BTim"][:, c, :], umq[:, 0:BLK])
                    tt(t1, pbr, cosT, ALU.mult)
                    tt(t2, pbi, sinT, ALU.mult)
                    tt(t3, pbi, cosT, ALU.mult)
                    tt(t4, pbr, sinT, ALU.mult)
                    tt(t1, t1, t2, ALU.add, eng="pool")
                    tt(t3, t3, t4, ALU.subtract, eng="pool")
                    magb = s5["mag"][:, q:q + 1].bc([128, BLK])
                    P.I("dve", "tensor_tensor_scan", out=t1, data0=magb, data1=t1, initial=s5st_re[l][:, q:q + 1], op0=ALU.mult, op1=ALU.add)
                    P.I("dve", "tensor_tensor_scan", out=t3, data0=magb, data1=t3, initial=s5st_im[l][:, q:q + 1], op0=ALU.mult, op1=ALU.add)
                    tt(t2, t3, sinT, ALU.mult)
                    tt(s_re, t1, cosT, ALU.mult)
                    tt(s_re, s_re, t2, ALU.subtract, eng="pool")
                    tt(rk, t3, cosT, ALU.mult, eng="pool")
                    tt(s_im, t1, sinT, ALU.mult, eng="pool")
                    tt(s_im, s_im, rk, ALU.add, eng="pool")
                    cp(s5st_re[l][:, q:q + 1], s_re[:, BLK - 1:BLK], eng="act")
                    cp(s5st_im[l][:, q:q + 1], s_im[:, BLK - 1:BLK], eng="act")
                    mm(ypb[rs, :], s5["CTre"][:, q, :], s_re, start=True, stop=False, tp=(0, 32 * ql))
                    mm(ypb[rs, :], s5["CTim"][:, q, :], s_im, start=False, stop=True, tp=(0, 32 * ql))
                    if has_s:
                        psr, psi = slot64[5][0], slot64[5][1]
                        mm(psr, s5["BTre"][:, c, :], umq[:, BLK:NT])
                        mm(psi, s5["BTim"][:, c, :], umq[:, BLK:NT])
                        ptr_, pti_ = slot64[5][2][:, 0:NSEQ], slot64[5][3][:, 0:NSEQ]
                        P.dma("sp", out=sst_re[q % 2], in_=st_re[l][:, q * 128:(q + 1) * 128])
                        P.dma("sp", out=sst_im[q % 2], in_=st_im[l][:, q * 128:(q + 1) * 128])
                        tr(ptr_, sst_re[q % 2], NSEQ)
                        tr(pti_, sst_im[q % 2], NSEQ)
                        cp(ss_re[0], ptr_, eng="act")
                        cp(ss_im[0], pti_, eng="act")
                        arc, aic = s5["abr"][:, q:q + 1], s5["abi"][:, q:q + 1]
                        sr3 = ss_re[1].re("p (s t) -> p s t", t=4)
                        si3 = ss_im[1].re("p (s t) -> p s t", t=4)
                        br3 = psr.re("p (s t) -> p s t", t=4)
                        bi3 = psi.re("p (s t) -> p s t", t=4)
                        for t in range(4):
                            pr = ss_re[0] if t == 0 else sr3[:, :, t - 1]
                            pi_ = ss_im[0] if t == 0 else si3[:, :, t - 1]
                            ts(stm[0], pi_, aic, ALU.mult)
                            stt(stm[0], pr, arc, stm[0], ALU.mult, ALU.subtract)
                            ts(stm[1], pr, aic, ALU.mult)
                            stt(stm[1], pi_, arc, stm[1], ALU.mult, ALU.add)
                            tt(sr3[:, :, t], stm[0], br3[:, :, t], ALU.add)
                            tt(si3[:, :, t], stm[1], bi3[:, :, t], ALU.add)
                        yps = slot64[7][0]
                        mm(yps[rs, :], s5["CTre"][:, q, :], ss_re[1], start=True, stop=False, tp=(0, 32 * ql))
                        mm(yps[rs, :], s5["CTim"][:, q, :], ss_im[1], start=False, stop=True, tp=(0, 32 * ql))
                        pfr, pfi = quarter[7][1][0:NSEQ, :], quarter[7][2][0:NSEQ, :]
                        tr(pfr, sr3[:, :, 3], 128)
                        tr(pfi, si3[:, :, 3], 128)
                        cp(so_re[q % 2], pfr, eng="act")
                        cp(so_im[q % 2], pfi, eng="act")
                        P.dma("sp", is_out=True, out=sre_s[l][:, q * 128:(q + 1) * 128], in_=so_re[q % 2])
                        P.dma("sp", is_out=True, out=sim_s[l][:, q * 128:(q + 1) * 128], in_=so_im[q % 2])
                stt(t2, uT[c][:, 0:BLK], ssmd_c[l][:, c:c + 1], bankV[6], ALU.mult, ALU.add)
                act(uT[c][:, 0:BLK], t2, AF.Gelu)
                if has_s:
                    stt(ytmp_s, uT[c][:, BLK:NT], ssmd_c[l][:, c:c + 1], slot64[7][0], ALU.mult, ALU.add)
                    act(uT[c][:, BLK:NT], ytmp_s, AF.Gelu)
            if blk == NBLK - 1:
                for (stc, dst) in ((s5st_re[l], sre_p[l]), (s5st_im[l], sim_p[l])):
                    pf = quarter[7][0][0:32, :]
                    tr(pf, stc, 128)
                    fo = ar.alloc(128, parts=32)
                    cp(fo, pf, eng="act")
                    P.dma("sp", is_out=True, out=dst, in_=fo)
            if ksub <= 3:
                return
            zT = uT
            gtmp = [t1, t3]
            stmp = [s_re, s_im]
            for m in range(8):
                wGl = wload(w_glu[l][:, m * 128:(m + 1) * 128], 8, 128).f()
                wGa = wload(w_in[l][:, O_GA + m * 128:O_GA + (m + 1) * 128], 16, 128)
                for gi, (col0, n) in enumerate(groups):
                    p1 = ps_for(gi, n)
                    for k in range(8):
                        mm(p1, wGl[:, k, :], zT[k][:, col0:col0 + n], start=(k == 0), stop=(k == 7))
                    p2 = (ps_aux()[:, 0:n] if gi == 0 else ps_s64()[:, 0:n])
                    for k in range(16):
                        mm(p2, wGa[:, k, :], xTr[k][:, col0:col0 + n], start=(k == 0), stop=(k == 15))
                    g = gtmp[gi][:, 0:n]
                    sg = stmp[gi][:, 0:n]
                    act(g, p1, AF.Sigmoid, bias=bglu_c[l][:, m:m + 1])
                    act(sg, p2, AF.Silu)
                    tt(g, g, zT[m][:, col0:col0 + n], ALU.mult)
                    tt(ya[m][:, col0:col0 + n], g, sg, ALU.mult)
            mtmp = [t1, t3]
            if ksub <= 4:
                return
            merge_branch(l, 0, w_pa[l], ya, groups)

        def phase_B(l, blk, groups):
            ar.reset()
            ar16.reset()
            global mtmp
            ybT = [ar16.alloc(NT) for _ in range(8)]
            has_s = len(groups) > 1
            ntile = 5 if has_s else 4
            vn16 = [ar16.alloc(1024) for _ in range(ntile)]
            vtok = [ar.alloc(1024) for _ in range(ntile)]
            bsb = ar.alloc(1024)
            mkB = ar.mark()
            lnvg = ar.alloc(1024)
            lnvb = ar.alloc(1024)
            P.dma("sp", out=lnvg, in_=ln_v_g[l].partition_broadcast(128))
            P.dma("sp", out=lnvb, in_=ln_v_b[l].partition_broadcast(128))
            P.dma("sp", out=bsb, in_=b_s[l].partition_broadcast(128))
            vf = [ar.alloc(BLK), ar.alloc(BLK)]
            cnt = [0]

            def cons_v(m, gi, ps, c0, n):
                t = vf[cnt[0] % 2]
                cnt[0] += 1
                cp(t[:, 0:n], ps, eng="act")
                if gi == 0:
                    pq = ps_aux()
                    for i in range(4):
                        tr(pq[:, i * 128:(i + 1) * 128], t[:, i * 128:(i + 1) * 128], 128)
                    for i in range(4):
                        cp(vtok[i][:, m * 128:(m + 1) * 128], pq[:, i * 128:(i + 1) * 128], eng="dve")
                else:
                    pq = quarter[7][m % 4][0:NS, :]
                    tr(pq, t[:, 0:NS], 128)
                    cp(vtok[4][0:NS, m * 128:(m + 1) * 128], pq, eng="dve")

            proj_fm(w_in[l][:, O_VB:O_VB + 1024], 16, 1024, xTr, groups, cons_v)
            stats = ar.alloc(12)
            mv = ar.alloc(2)
            for i in range(ntile):
                n = 128 if i < 4 else NS
                v = vtok[i][0:n, :]
                st_ = stats[0:n, :].re("p (a b) -> p a b", b=6)
                P.I("dve", "bn_stats", out=st_[:, 0, :], in_=v[:, 0:512])
                P.I("dve", "bn_stats", out=st_[:, 1, :], in_=v[:, 512:1024])
                P.I("dve", "bn_aggr", out=mv[0:n, :], in_=st_)
                act(mv[0:n, 1:2], mv[0:n, 1:2], AF.Sqrt, bias=eps_ln[0:n, :])
                P.I("dve", "reciprocal", out=mv[0:n, 1:2], in_=mv[0:n, 1:2])
                ts(v, v, mv[0:n, 0:1], ALU.subtract, mv[0:n, 1:2], ALU.mult)
                tt(v, v, lnvg[0:n, :], ALU.mult)
                if i < 4:
                    tt(vn16[i][0:n, :], v, lnvb[0:n, :], ALU.add)
                else:
                    tt(v, v, lnvb[0:n, :], ALU.add)
                    P.dma("sp", is_out=True, out=gv_s[l], in_=v)
                    cp(vn16[i][0:n, :], v, eng="dve")
            ar.release(mkB)
            wst = ar.alloc(128)
            wsT = [V(ws_t[:, g, :], [Res()]) for g in range(8)] if not ws_cache else ws_cache[0]
            wsTs = [V(wss_t[:, g, :], [Res()]) for g in range(8)] if not ws_cache else ws_cache[1]
            if not ws_cache:
                ws_cache.extend([wsT, wsTs])
            y4 = ar.alloc(64, parts=4)
            wtmp = ar.alloc(128)
            for g in range(8):
                P.dma("sp", out=wst, in_=w_s[l, g])
                pq = quarter[7][g % 4]
                tr(pq, wst, 128)
                tt(wtmp, pq, mask_ui, ALU.mult)
                cp(wsT[g], wtmp, eng="dve")
                if has_s:
                    cp(y4.re("p (b i) -> p b i", i=4), wtmp[0:4, 0:4].un(1).bc([4, 16, 4]), eng="dve")
                    p64 = slot64[7][4 + g % 4][0:64, :]
                    mm(p64, rep4, y4)
                    tt(wsTs[g], p64, bm64, ALU.mult)
            tA = [ar.alloc(BLK), ar.alloc(BLK)]
            tB = [ar.alloc(BLK), ar.alloc(BLK)]
            for g in range(8):
                gc_ = slice(g * 128, (g + 1) * 128)
                wU = wload(w_in[l][:, O_UB + g * 128:O_UB + (g + 1) * 128], 16, 128)
                wG = wload(w_in[l][:, O_GB + g * 128:O_GB + (g + 1) * 128], 16, 128)
                for gi, (col0, n) in enumerate(groups):
                    if gi == 0:
                        sps = bankV[5]
                        for i in range(4):
                            mm(sps[:, i * 128:(i + 1) * 128], vn16[i][:, gc_], wsT[g])
                        pu, pg = ps_main(), ps_aux()
                        bias_v = bsb[:, gc_].un(1).bc([128, 4, 128])
                        v3 = lambda v: v.re("p (n k) -> p n k", k=128)
                    else:
                        sps = slot64[6][g % 8]
                        mm(sps, vn16[4][0:NS, gc_], wsTs[g])
                        pu, pg = ps_s64(), ps_s64()
                        bias_v = bsb[:, g * 128:g * 128 + 4].un(1).bc([128, 16, 4])
                        v3 = lambda v: v.re("p (s t) -> p s t", t=4)
                    for k in range(16):
                        mm(pu[:, 0:n], wU[:, k, :], xTr[k][:, col0:col0 + n], start=(k == 0), stop=(k == 15))
                    for k in range(16):
                        mm(pg[:, 0:n], wG[:, k, :], xTr[k][:, col0:col0 + n], start=(k == 0), stop=(k == 15))
                    a_, b_ = tA[gi][:, 0:n], tB[gi][:, 0:n]
                    act(a_, pg[:, 0:n], AF.Silu)
                    tt(v3(b_), v3(sps[:, 0:n]), bias_v, ALU.add)
                    tt(b_, b_, pu[:, 0:n], ALU.mult)
                    tt(ybT[g][:, col0:col0 + n], b_, a_, ALU.mult)
            mtmp = [ar.alloc(BLK), ar.alloc(BLK)]
            merge_branch(l, 1, w_pb[l], ybT, groups)

        def gdn_items(items, slots):
            n_it = len(items)
            for i, it in enumerate(items):
                it["s"] = slots[i % len(slots)]
            for it in items:
                T, s = it["T"], it["s"]
                pq = s["pq8"][0:4]
                gmat, bmat = s["t0"][0:T, :], s["t1"][0:T, :]
                ts(gmat, ones[0:T, :], it["gcol"], ALU.mult)
                ts(bmat[:, 0:T], ones[0:T, 0:T], it["bcol"], ALU.mult)
                mm(pq[0][0:T, 0:1], mask_ui[0:T, 0:T], it["gcol"])
                mm(pq[1][:, 0:T], gmat, mask_ui[0:T, 0:T])
                mm(pq[2][0:T, 0:T], bmat[:, 0:T], ident[0:T, 0:T])
                mm(pq[3][0:T, 0:T], it["kT"], it["kT"])
                cp(s["gccol"][0:T, :], pq[0][0:T, 0:1], eng="act")
                cp(s["gcrow"][:, 0:T], pq[1][:, 0:T], eng="act")
            for it in items:
                T, s = it["T"], it["s"]
                pq = s["pq8"][0:4]
                dT = s["t0"][0:T, 0:T]
                ts(dT, s["gcrow"][0:T, 0:T], s["gccol"][0:T, :], ALU.subtract, 0.0, ALU.min)
                act(dT, dT, AF.Exp)
                tt(s["decm"][0:T, 0:T], dT, mask_ui[0:T, 0:T], ALU.mult)
                Y = s["Y"][0:T, 0:T]
                tt(Y, dT, negmask_su[0:T, 0:T], ALU.mult)
                tt(Y, Y, pq[2][0:T, 0:T], ALU.mult)
                tt(Y, Y, pq[3][0:T, 0:T], ALU.mult)
                sc = s["sc"]
                act(sc[0:T, 0:1], s["gccol"][0:T, :], AF.Exp)
                tt(sc[0:T, 1:2], sc[0:T, 0:1], it["bcol"], ALU.mult)
                act(sc[0:T, 2:3], s["gccol"][0:T, :], AF.Exp, bias=s["gcrow"][0:T, T - 1:T], scale=-1.0)
                act(sc[:, 3:4], s["gcrow"][:, T - 1:T], AF.Exp)
                act(s["egrow"][:, 0:T], s["gcrow"][:, 0:T], AF.Exp)
            for it in items:
                T, s = it["T"], it["s"]
                pq = s["pq8"][4:8]
                tr(pq[0][0:T, 0:T], s["Y"][0:T, 0:T], T)
                cp(s["A"][0:T, 0:T], pq[0][0:T, 0:T], eng="act")
                tt(s["MT"][0:T, 0:T], s["Y"][0:T, 0:T], ident[0:T, 0:T], ALU.add)
                it["X"], it["XT"] = s["Y"], s["A"]
                it["pp"] = 0
            nsteps = max(int(math.ceil(math.log2(it["T"]))) - 1 for it in items)
            for stp in range(nsteps):
                for it in items:
                    T, s = it["T"], it["s"]
                    if stp >= int(math.ceil(math.log2(T))) - 1:
                        continue
                    pq = s["pq8"][(stp % 2) * 4:(stp % 2) * 4 + 4]
                    X, XT = it["X"][0:T, 0:T], it["XT"][0:T, 0:T]
                    X2s, XT2s = (s["X1"], s["XT1"]) if it["pp"] == 0 else (s["X0"], s["XT0"])
                    it["pp"] ^= 1
                    mm(pq[1][0:T, 0:T], XT, X)
                    mm(pq[2][0:T, 0:T], X, XT)
                    cp(X2s[0:T, 0:T], pq[1][0:T, 0:T], eng="act")
                    cp(XT2s[0:T, 0:T], pq[2][0:T, 0:T], eng="dve")
                    it["X"], it["XT"] = X2s, XT2s
                for it in items:
                    T, s = it["T"], it["s"]
                    if stp >= int(math.ceil(math.log2(T))) - 1:
                        continue
                    pq = s["pq8"][(stp % 2) * 4:(stp % 2) * 4 + 4]
                    mm(pq[3][0:T, 0:T], it["XT"][0:T, 0:T], s["MT"][0:T, 0:T])
                    tt(s["MT"][0:T, 0:T], s["MT"][0:T, 0:T], pq[3][0:T, 0:T], ALU.add)
            for it in items:
                T, s = it["T"], it["s"]
                pq = s["pq8"][0:4]
                sc = s["sc"]
                vb_, kbg, kdec = s["t0"][0:T, :], s["t1"][0:T, :], s["kdec"][0:T, :]
                ts(vb_, it["vtok"], it["bcol"], ALU.mult)
                ts(kbg, it["ktok"], sc[0:T, 1:2], ALU.mult)
                ts(kdec, it["ktok"], sc[0:T, 2:3], ALU.mult)
                MT = s["MT"][0:T, 0:T]
                mm(pq[0][0:T, :], MT, vb_)
                mm(pq[1][:, 0:T], kbg, MT)
                mm(pq[2][0:T, 0:T], it["kT"], it["qT"])
                cp(s["X0"][0:T, :], pq[0][0:T, :], eng="act")
                cp(s["XT0"][:, 0:T], pq[1][:, 0:T], eng="act")
                tt(s["X1"][0:T, 0:T], pq[2][0:T, 0:T], s["decm"][0:T, 0:T], ALU.mult)
                tt(s["XT1"][:, 0:T], it["qT"], s["egrow"][:, 0:T], ALU.mult)
            for it in items:
                T, s = it["T"], it["s"]
                pq = s["pq8"][4:8]
                sc = s["sc"]
                S = it["S"]
                mm(pq[0][0:T, :], s["XT0"][:, 0:T], S)
                vnew = s["Y"][0:T, :]
                tt(vnew, s["X0"][0:T, :], pq[0][0:T, :], ALU.subtract)
                mm(pq[1][:, 0:T], S, s["XT1"][:, 0:T], start=True, stop=False)
                mm(pq[1][:, 0:T], vnew, s["X1"][0:T, 0:T], start=False, stop=True)
                mm(pq[2], s["kdec"][0:T, :], vnew)
                stt(S, S, sc[:, 3:4], pq[2], ALU.mult, ALU.add)
            for it in items:
                T, s = it["T"], it["s"]
                pq = s["pq8"][4:8]
                sq = s["A"][:, 0:T]
                act(sq, pq[1][:, 0:T], AF.Square)
                mm(pq[3][:, 0:T], ones, sq)
            for it in items:
                T, s = it["T"], it["s"]
                pq = s["pq8"][4:8]
                rstd = s["A"][:, 0:T]
                act(rstd, pq[3][:, 0:T], AF.Ln, bias=eps_n, scale=1.0 / 128.0)
                act(rstd, rstd, AF.Exp, scale=-0.5)
                tt(rstd, rstd, pq[1][:, 0:T], ALU.mult)
                stt(it["odst"], rstd, it["ncol"], it["sg"], ALU.mult, ALU.mult)

        def make_slots(nslots, bank0):
            sl = []
            for i in range(nslots):
                d = {}
                for nm in ("t0", "t1", "gcrow", "egrow", "decm", "Y", "A", "X0", "XT0", "X1", "XT1", "MT", "kdec"):
                    d[nm] = ar.alloc(128)
                d["gccol"] = ar.alloc(1)
                d["sc"] = ar.alloc(4)
                d["pq8"] = [quarter[bank0 + i][qq] for qq in range(4)] * 2
                d["ri"] = [0]
                sl.append(d)
            return sl

        def phase_C(l, blk, groups):
            ar.reset()
            ar16.reset()
            global mtmp
            has_s = len(groups) > 1
            ycT = [ar16.alloc(NT) for _ in range(8)]
            ntile = 5 if has_s else 4
            gtok = ar.alloc(8 * ntile)
            btok = ar.alloc(8 * ntile)
            atmp = ar.alloc(8)
            wab = wload(w_in[l][:, O_AC:O_AC + 16], 16, 16)
            for i in range(ntile):
                n = 128 if i < 4 else NS
                c0 = i * 128
                ps = quarter[7][i % 4][0:n, 0:16]
                for k in range(16):
                    mm(ps, xTr[k][:, c0:c0 + n], wab[:, k, :], start=(k == 0), stop=(k == 15))
                tt(atmp[0:n, :], ps[:, 0:8], dtb[l][0:n, :], ALU.add)
                act(atmp[0:n, :], atmp[0:n, :], AF.Exp)
                act(atmp[0:n, :], atmp[0:n, :], AF.Ln, bias=one_c[0:n, :])
                tt(gtok[0:n, i * 8:(i + 1) * 8], atmp[0:n, :], negea[l][0:n, :], ALU.mult)
                act(btok[0:n, i * 8:(i + 1) * 8], ps[:, 8:16], AF.Sigmoid)
            xpp = [ar.alloc(NT)] * 3
            qa = [ar.alloc(NT) for _ in range(3)]
            sq = xpp[0]
            sgate = ar.alloc(NT)
            ktok = [ar.alloc(128) for _ in range(4)]
            vtk = [ar.alloc(128) for _ in range(4)]
            if has_s:
                xps = [ar.alloc(NSEQ * 7)] * 3
                cst1 = ar.alloc(128, parts=48)
                ktk_s = [x[0:4, :] for x in ktok]
                vtk_s = [x[0:4, :] for x in vtk]
                Ss = [ar.alloc(128) for _ in range(4)]
                cvst = ar.alloc(128, parts=NS)
                gts_all = ar.alloc(128, parts=4).re("p (s e) -> p s e", e=8)
                bts_all = ar.alloc(128, parts=4).re("p (s e) -> p s e", e=8)
                for sq_ in range(NSEQ):
                    pg_ = slot64[7][2 + sq_ % 2][0:4, 0:16]
                    mm(pg_[:, 0:8], ident[0:NS, 4 * sq_:4 * sq_ + 4], gtok[0:NS, 32:40])
                    mm(pg_[:, 8:16], ident[0:NS, 4 * sq_:4 * sq_ + 4], btok[0:NS, 32:40])
                    cp(gts_all[:, sq_, :], pg_[:, 0:8], eng="act")
                    cp(bts_all[:, sq_, :], pg_[:, 8:16], eng="dve")
            if blk == NBLK - 1:
                cvpt = ar.alloc(128, parts=3)
            slots = make_slots(4, 4)
            for h in range(HC):
                wq = [wload(w_in[l][:, O_QKV + j * 1024 + h * 128:O_QKV + j * 1024 + (h + 1) * 128], 16, 128) for j in range(3)]
                for j in range(3):
                    cc = j * 8 + h
                    if has_s:
                        P.dma("sp", out=cst1, in_=st_cv[l][:, j * 1024 + h * 128:j * 1024 + (h + 1) * 128])
                    for gi, (col0, n) in enumerate(groups):
                        ps = ps_for(gi, n)
                        for k in range(16):
                            mm(ps, wq[j][:, k, :], xT[k][:, col0:col0 + n].r(), start=(k == 0), stop=(k == 15))
                        cw = lambda t: convw_c[l][:, t * 24 + cc:t * 24 + cc + 1]
                        if gi == 0:
                            cp(xpp[j][:, 0:3], ctail[l][cc], eng="dve")
                            cp(xpp[j][:, 3:3 + BLK], ps, eng="act")
                            cp(ctail[l][cc], xpp[j][:, BLK:BLK + 3], eng="dve")
                            acc = qa[j][:, 0:BLK]
                            ts(acc, xpp[j][:, 0:BLK], cw(0), ALU.mult)
                            for t in range(1, 4):
                                stt(acc, xpp[j][:, t:t + BLK], cw(t), acc, ALU.mult, ALU.add)
                            act(acc, acc, AF.Silu)
                        else:
                            x3 = xps[j].re("p (s t) -> p s t", t=7)
                            pt = slot64[7][4 + j][:, 0:48]
                            tr(pt, cst1, 48)
                            cp(x3[:, :, 0:3], pt.re("p (s t) -> p s t", t=3), eng="dve")
                            cp(x3[:, :, 3:7], ps.re("p (s t) -> p s t", t=4), eng="act")
                            acc = qa[j][:, BLK:NT].re("p (s t) -> p s t", t=4)
                            ts(acc, x3[:, :, 0:4], cw(0), ALU.mult)
                            for t in range(1, 4):
                                stt(acc, x3[:, :, t:t + 4], cw(t), acc, ALU.mult, ALU.add)
                            act(qa[j][:, BLK:NT], qa[j][:, BLK:NT], AF.Silu)
                            pc_ = slot64[7][7][0:NS, :]
                        if gi == 1:
                            for half in range(2):
                                pr_ = slot64[6][half][0:NS, :]
                                for k in range(16):
                                    mm(pr_, xT[k][:, BLK:NT].r(), wq[j][:, k, half * 64:(half + 1) * 64], start=(k == 0), stop=(k == 15))
                                cp(cvst[:, half * 64:(half + 1) * 64], pr_, eng="act")
                            for t in range(1, 4):
                                P.dma("sp", is_out=True, out=cv_s[l][:, t - 1, j * 1024 + h * 128:j * 1024 + (h + 1) * 128], in_=cvst[t:NS:4, :])
                        if gi == 0 and blk == NBLK - 1:
                            for half in range(2):
                                pr_ = slot64[6][2 + half][0:3, :]
                                for k in range(16):
                                    mm(pr_, xT[k][:, BLK - 3:BLK].r(), wq[j][:, k, half * 64:(half + 1) * 64], start=(k == 0), stop=(k == 15))
                                cp(cvpt[:, half * 64:(half + 1) * 64], pr_, eng="act")
                            P.dma("sp", is_out=True, out=cv_p[l][:, j * 1024 + h * 128:j * 1024 + (h + 1) * 128], in_=cvpt)
                for j in range(2):
                    for gi, (col0, n) in enumerate(groups):
                        act(sq[:, col0:col0 + n], qa[j][:, col0:col0 + n], AF.Square)
                        ps = ps_for(gi, n)
                        mm(ps, ones, sq[:, col0:col0 + n])
                        act(sq[:, col0:col0 + n], ps, AF.Ln, bias=eps_n)
                        act(sq[:, col0:col0 + n], sq[:, col0:col0 + n], AF.Exp, scale=-0.5)
                        if j == 0:
                            stt(qa[j][:, col0:col0 + n], qa[j][:, col0:col0 + n], 128.0 ** -0.5, sq[:, col0:col0 + n], ALU.mult, ALU.mult)
                        else:
                            tt(qa[j][:, col0:col0 + n], qa[j][:, col0:col0 + n], sq[:, col0:col0 + n], ALU.mult)
                wg = wload(w_in[l][:, O_GC + h * 128:O_GC + (h + 1) * 128], 16, 128)
                for gi, (col0, n) in enumerate(groups):
                    ps = ps_for(gi, n)
                    for k in range(16):
                        mm(ps, wg[:, k, :], xT[k][:, col0:col0 + n].r(), start=(k == 0), stop=(k == 15))
                    act(sgate[:, col0:col0 + n], ps, AF.Silu)
                pk, pv = bankV[3], bankV[3]
                for sb in range(4):
                    cs = slice(sb * 128, (sb + 1) * 128)
                    tr(quarter[3][sb], qa[1][:, cs], 128)
                    cp(ktok[sb], quarter[3][sb], eng="act")
                for sb in range(4):
                    cs = slice(sb * 128, (sb + 1) * 128)
                    tr(quarter[3][sb], qa[2][:, cs], 128)
                    cp(vtk[sb], quarter[3][sb], eng="dve")
                items = []
                for sb in range(4):
                    cs = slice(sb * 128, (sb + 1) * 128)
                    items.append(dict(T=128, kT=qa[1][:, cs], qT=qa[0][:, cs], vT=qa[2][:, cs], ktok=ktok[sb], vtok=vtk[sb],
                                      gcol=gtok[:, sb * 8 + h:sb * 8 + h + 1], bcol=btok[:, sb * 8 + h:sb * 8 + h + 1], S=gS[l][h],
                                      odst=ycT[h][:, cs], sg=sgate[:, cs], ncol=normg_c[l][:, 0:1]))
                gdn_items(items, slots)
                if blk == NBLK - 1:
                    P.dma("sp", is_out=True, out=dl_p[l, h], in_=gS[l][h])
                if has_s:
                    for s0 in range(0, NSEQ, 4):
                        its = []
                        for si in range(4):
                            sq_ = s0 + si
                            c4 = slice(BLK + sq_ * 4, BLK + sq_ * 4 + 4)
                            pkq = slot64[3][si][0:4, :]
                            P.dma("sp", out=Ss[si], in_=st_dl[l, sq_, h])
                            tr(quarter[3][si][0:4, :], qa[1][:, c4], 128)
                            cp(ktk_s[si], quarter[3][si][0:4, :], eng="act")
                        for si in range(4):
                            sq_ = s0 + si
                            c4 = slice(BLK + sq_ * 4, BLK + sq_ * 4 + 4)
                            tr(quarter[3][si][0:4, :], qa[2][:, c4], 128)
                            cp(vtk_s[si], quarter[3][si][0:4, :], eng="dve")
                            r4 = slice(sq_ * 4, sq_ * 4 + 4)
                            its.append(dict(T=4, kT=qa[1][:, c4], qT=qa[0][:, c4], vT=qa[2][:, c4], ktok=ktk_s[si], vtok=vtk_s[si],
                                            gcol=gts_all[:, sq_, h:h + 1], bcol=bts_all[:, sq_, h:h + 1], S=Ss[si],
                                            odst=ycT[h][:, c4], sg=sgate[:, c4], ncol=normg_c[l][:, 0:1]))
                        gdn_items(its, slots)
                        for si in range(4):
                            P.dma("sp", is_out=True, out=dl_s[l, s0 + si, h], in_=Ss[si])
            mtmp = [sq[:, 0:BLK], sgate[:, 0:BLK]]
            merge_branch(l, 2, w_pc[l], ycT, groups)

        def phase_D(l, blk, groups):
            ar.reset()
            ar16.reset()
            last = (l == NL - 1)
            rT = [ar.alloc(NT) for _ in range(16)]
            sqt = [ar.alloc(BLK), ar.alloc(NS)]
            mean_t = [ar.alloc(BLK), ar.alloc(NS)]
            rstd_t = [ar.alloc(BLK), ar.alloc(NS)]
            psum_s = [bankV[5], slot64[7][0]]
            psum_q = [bankV[6], slot64[7][1]]
            for m in range(16):
                w = wload(w_out[l][:, m * 128:(m + 1) * 128], 16, 128)
                for gi, (col0, n) in enumerate(groups):
                    ps = ps_for(gi, n)
                    for k in range(16):
                        mm(ps, w[:, k, :], mg[k][:, col0:col0 + n].r(), start=(k == 0), stop=(k == 15))
                    rm = rT[m][:, col0:col0 + n]
                    stt(rm, xT[m][:, col0:col0 + n], ALPHA, ps, ALU.mult, ALU.add)
                    sq_ = sqt[gi][:, 0:n]
                    act(sq_, rm, AF.Square)
                    mm(psum_s[gi][:, 0:n], ones, rm, start=(m == 0), stop=(m == 15))
                    mm(psum_q[gi][:, 0:n], ones, sq_, start=(m == 0), stop=(m == 15))
            for gi, (col0, n) in enumerate(groups):
                me, rs_ = mean_t[gi][:, 0:n], rstd_t[gi][:, 0:n]
                act(me, psum_s[gi][:, 0:n], AF.Copy, scale=1.0 / D)
                act(rs_, me, AF.Square)
                stt(rs_, psum_q[gi][:, 0:n], 1.0 / D, rs_, ALU.mult, ALU.subtract)
                act(rs_, rs_, AF.Sqrt, bias=eps_ln)
                P.I("dve", "reciprocal", out=rs_, in_=rs_)
                for m in range(16):
                    rm = rT[m][:, col0:col0 + n]
                    tt(rm, rm, me, ALU.subtract)
                    tt(rm, rm, rs_, ALU.mult)
                    if last:
                        ts(rm, rm, lng_c[l][:, m:m + 1], ALU.mult, lnb_c[l][:, m:m + 1], ALU.add)
                    else:
                        ts(xT[m][:, col0:col0 + n].r(), rm, lng_c[l][:, m:m + 1], ALU.mult, lnb_c[l][:, m:m + 1], ALU.add)
            if last:
                ystg = [ar.alloc(BLK), ar.alloc(BLK)]
                tiles = [(y_p[blk * BLK + i * 128: blk * BLK + (i + 1) * 128, :], i * 128, 128) for i in range(4)]
                if len(groups) > 1:
                    tiles.append((y_s, BLK, NS))
                ci = 0
                for ti, (dst, col0, n) in enumerate(tiles):
                    for k4 in range(4):
                        s_ = ystg[ci % 2]
                        ci += 1
                        ps = ps_main()
                        for j in range(4):
                            k = k4 * 4 + j
                            tr(ps[0:n, j * 128:(j + 1) * 128], rT[k][:, col0:col0 + n], 128)
                        cp(s_[0:n, :], ps[0:n, :], eng=("act" if k4 % 2 == 0 else "dve"))
                        P.dma("sp", is_out=True, out=dst[:, k4 * 512:(k4 + 1) * 512], in_=s_[0:n, :])

        krow512 = misc_alloc(BLK)
        halfpi = misc_alloc(1)
        P.I("pool", "iota", out=krow512, pattern=[[1, BLK]], base=1, channel_multiplier=0, allow_small_or_imprecise_dtypes=True)
        P.I("pool", "memset", ap=halfpi, constant=math.pi / 2)
        eps_ln = misc_alloc(1)
        eps_n = misc_alloc(1)
        one_c = misc_alloc(1)
        P.I("pool", "memset", ap=eps_ln, constant=1e-5)
        P.I("pool", "memset", ap=eps_n, constant=1e-6)
        P.I("pool", "memset", ap=one_c, constant=1.0)
        ws_cache = []

        import os
        kstop = int(os.environ.get("KSTOP", "100000"))
        nph = [0]

        def go(fn, *a):
            if nph[0] < kstop:
                fn(*a)
            nph[0] += 1

        for blk in range(NBLK):
            groups = [(0, BLK)] + ([(BLK, NS)] if blk == 0 else [])
            go(load_x, blk, groups)
            for l in range(NL):
                go(phase_A, l, blk, groups)
                go(phase_B, l, blk, groups)
                go(phase_C, l, blk, groups)
                go(phase_D, l, blk, groups)
        P.emit()
    return nc


_CACHE = {}


def kernel(**inp):
    f = lambda a: np.ascontiguousarray(np.asarray(a, dtype=np.float32))
    if "nc" not in _CACHE:
        _CACHE["nc"] = build_program()
    nc = _CACHE["nc"]
    shared = dict(
        w_in=f(inp["w_in"]), b_gate=f(inp["b_gate"]).reshape(NL, 48, 128),
        lam_re=f(inp["lam_re"]).reshape(NL, 32, 128), lam_im=f(inp["lam_im"]).reshape(NL, 32, 128),
        log_dt=f(inp["log_dt"]).reshape(NL, 32, 2),
        b_re=f(inp["ssm_b_re"]).reshape(NL, 4096, 16), b_im=f(inp["ssm_b_im"]).reshape(NL, 4096, 16),
        c_re=f(inp["ssm_c_re"]).reshape(NL, 1024, 64), c_im=f(inp["ssm_c_im"]).reshape(NL, 1024, 64),
        ssm_d=f(inp["ssm_d"]).reshape(NL, 8, 128), w_glu=f(inp["w_glu"]), b_glu=f(inp["b_glu"]).reshape(NL, 8, 128),
        w_pa=f(inp["w_pa"]), ln_v_g=f(inp["ln_v_g"]), ln_v_b=f(inp["ln_v_b"]), w_s=f(inp["w_s"]),
        b_s=f(inp["b_s"]).reshape(NL, 1024), w_pb=f(inp["w_pb"]), conv_w=f(inp["conv_w"]).reshape(NL, 96, 128),
        a_log=f(inp["a_log"]), dt_bias=f(inp["dt_bias"]), norm_g=f(inp["gdn_norm_g"]).reshape(NL, 1, 128),
        w_pc=f(inp["w_pc"]), w_out=f(inp["w_out"]), ln_g=f(inp["ln_g"]).reshape(NL, 16, 128),
        ln_b=f(inp["ln_b"]).reshape(NL, 16, 128),
    )
    xpr = f(inp["x_prompt"])
    xsm = f(inp["x_sample"])
    sre = f(inp["state_ssm_re"])
    sim = f(inp["state_ssm_im"])
    sdl = f(inp["state_delta"])
    scv = f(inp["state_conv"])
    in_maps = []
    for c in range(8):
        sl = slice(c * NSEQ, (c + 1) * NSEQ)
        m = dict(shared)
        m["xp"] = xpr[c % 4]
        m["xs"] = np.ascontiguousarray(xsm[sl].reshape(NS, D))
        m["st_re"] = np.ascontiguousarray(sre[:, sl].reshape(NL, NSEQ, 4096))
        m["st_im"] = np.ascontiguousarray(sim[:, sl].reshape(NL, NSEQ, 4096))
        m["st_dl"] = np.ascontiguousarray(sdl[:, sl])
        m["st_cv"] = np.ascontiguousarray(scv[:, sl].reshape(NL, NSEQ * 3, QKV))
        in_maps.append(m)
    import os
    ncores = int(os.environ.get("KCORES", "8"))
    res = run_bass_kernel_spmd(nc, in_maps[:ncores], core_ids=list(range(ncores))).results
    if ncores < 8:
        res = list(res) + [res[0]] * (8 - ncores)
    y_prompt = np.stack([res[c]["y_p"] for c in range(4)])
    y_sample = np.concatenate([res[c]["y_s"].reshape(NSEQ, 4, D) for c in range(8)], 0)
    ssm_re_p = np.stack([res[c]["sre_p"].reshape(NL, 64, 64) for c in range(4)], 1)
    ssm_im_p = np.stack([res[c]["sim_p"].reshape(NL, 64, 64) for c in range(4)], 1)
    delta_p = np.stack([res[c]["dl_p"] for c in range(4)], 1)
    conv_p = np.stack([res[c]["cv_p"] for c in range(4)], 1)
    ssm_re_s = np.concatenate([res[c]["sre_s"].reshape(NL, NSEQ, 64, 64) for c in range(8)], 1)
    ssm_im_s = np.concatenate([res[c]["sim_s"].reshape(NL, NSEQ, 64, 64) for c in range(8)], 1)
    delta_s = np.concatenate([res[c]["dl_s"] for c in range(8)], 1)
    conv_s = np.concatenate([res[c]["cv_s"] for c in range(8)], 1)
    gv = np.concatenate([res[c]["gv_s"].reshape(NL, NSEQ, 4, 1024) for c in range(8)], 1)
    return tuple(np.ascontiguousarray(a, dtype=np.float32) for a in
                 (y_prompt, y_sample, ssm_re_p, ssm_im_p, delta_p, conv_p, ssm_re_s, ssm_im_s, delta_s, conv_s, gv))
```

```python
import contextlib
import math
import numpy as np
import concourse.bass as bass
import concourse.mybir as mybir
from concourse.bass_utils import run_bass_kernel_spmd

F32 = mybir.dt.float32
F32R = mybir.dt.float32r
BF16 = mybir.dt.bfloat16
AF = mybir.ActivationFunctionType
ALU = mybir.AluOpType

D = 2048
SEQ = 2048
NL = 2
NIN = 15376
QKV = 3072
HC = 8
BLK = 512
NBLK = SEQ // BLK
NSEQ = 16
NS = NSEQ * 4
NT = BLK + NS
O_XA, O_GA, O_UB, O_VB, O_GB, O_QKV, O_AC, O_GC, O_ZG = 0, 1024, 2048, 3072, 4096, 5120, 8192, 8208, 9232
ALPHA = (2 * NL) ** 0.25
MAGIC = 12582912.0
TWO_PI = 2.0 * math.pi

ENGS = ("pe", "act", "dve", "pool", "sp")
EPOCH = 12000
N_DMA_SEMS = 40
DMA_DESC_LIMIT = 6144
WRITE_KEYS = ("out", "ap", "accum_out")


class Res:
    __slots__ = ("lw", "rd", "excl")

    def __init__(self, excl=False):
        self.lw = None
        self.rd = []
        self.excl = excl


class V:
    __slots__ = ("ap", "res")

    def __init__(self, ap, res):
        self.ap = ap
        self.res = tuple(res)

    def __getitem__(self, k):
        return V(self.ap[k], self.res)

    def r(self):
        return V(self.ap.bitcast(F32R), self.res)

    def f(self):
        return V(self.ap.bitcast(F32), self.res)

    def bc(self, shape):
        return V(self.ap.to_broadcast(list(shape)), self.res)

    def re(self, pat_, **kw):
        return V(self.ap.rearrange(pat_, **kw), self.res)

    def un(self, ax):
        return V(self.ap.unsqueeze(ax), self.res)


class Op:
    __slots__ = ("eng", "meth", "kw", "deps", "is_dma", "sig", "sigidx", "dsem", "dval")


class Prog:
    def __init__(self, nc):
        self.nc = nc
        self.ops = []
        self.per_eng = {e: [] for e in ENGS}
        self.dma_rr = 0
        self.dma_last = [None] * N_DMA_SEMS
        self.dma_cnt = [0] * N_DMA_SEMS
        self.out_dmas = []
        self.inflight = {e: [] for e in ENGS}

    def _add(self, eng, meth, kw, is_dma):
        op = Op()
        op.eng, op.meth, op.kw, op.is_dma = eng, meth, kw, is_dma
        op.deps = set()
        op.sig = False
        op.sigidx = None
        op.dsem = None
        op.dval = None
        reads, writes = [], []
        for k, v in kw.items():
            if isinstance(v, V):
                for r_ in v.res:
                    (writes if (k in WRITE_KEYS or r_.excl) else reads).append(r_)
        for r in reads:
            if r.lw is not None:
                op.deps.add(r.lw)
        for w in writes:
            if w.lw is not None:
                op.deps.add(w.lw)
            op.deps.update(w.rd)
        for r in reads:
            r.rd.append(op)
        for w in writes:
            w.lw = op
            w.rd = []
        if is_dma:
            s = self.dma_rr % N_DMA_SEMS
            self.dma_rr += 1
            if self.dma_last[s] is not None:
                op.deps.add(self.dma_last[s])
            self.dma_last[s] = op
            self.dma_cnt[s] += 16
            op.dsem = s
            op.dval = self.dma_cnt[s]
        op.deps.discard(op)
        self.ops.append(op)
        self.per_eng[eng].append(op)
        return op

    def I(self, eng, meth, **kw):
        return self._add(eng, meth, kw, False)

    @staticmethod
    def _ndesc(a):
        ap = a.ap if isinstance(a, V) else a
        dims = [list(d) for d in ap.ap]
        n = 1
        for (st_, sz) in dims:
            n *= sz
        if dims and dims[-1][0] == 1:
            n //= dims[-1][1]
        return max(1, n)

    def dma(self, eng, is_out=False, **kw):
        nd = max(self._ndesc(kw["out"]), self._ndesc(kw["in_"]))
        fl = self.inflight["sp"]
        extra = []
        while fl and sum(x[1] for x in fl) + nd > DMA_DESC_LIMIT:
            extra.append(fl.pop(0)[0])
        o = self._add(eng, "dma_start", kw, True)
        o.deps.update(extra)
        fl.append((o, nd))
        if is_out:
            self.out_dmas.append(o)
        return o

    def emit(self):
        nc = self.nc
        fin = Op()
        fin.eng, fin.meth, fin.kw, fin.is_dma = "sp", None, {}, False
        fin.deps = set(self.out_dmas)
        fin.sig = False
        fin.sigidx = None
        self.ops.append(fin)
        self.per_eng["sp"].append(fin)
        for op in self.ops:
            for d in op.deps:
                if not d.is_dma:
                    if d.eng == "pe" and op.eng == "pe" and not op.is_dma:
                        continue
                    d.sig = True
        nsig = {}
        for e in ENGS:
            c = 0
            for op in self.per_eng[e]:
                if op.sig and not op.is_dma:
                    c += 1
                    op.sigidx = c
            nsig[e] = c
        with contextlib.ExitStack() as st:
            esems = {}
            for e in ENGS:
                n_ep = max(1, (nsig[e] + EPOCH - 1) // EPOCH)
                esems[e] = [st.enter_context(nc.semaphore(f"s_{e}_{i}")) for i in range(n_ep)]
            dsems = [st.enter_context(nc.semaphore(f"s_dma_{i}")) for i in range(N_DMA_SEMS)]
            block = st.enter_context(nc.Block())

            def run(e, eng):
                seen = {}
                for op in self.per_eng[e]:
                    waits = {}
                    for d in op.deps:
                        if d.is_dma:
                            key, val, sem = ("d", d.dsem), d.dval, dsems[d.dsem]
                        else:
                            if d.eng == "pe" and e == "pe" and not op.is_dma:
                                continue
                            ep = (d.sigidx - 1) // EPOCH
                            key, val, sem = (d.eng, ep), d.sigidx - ep * EPOCH, esems[d.eng][ep]
                        if seen.get(key, 0) >= val:
                            continue
                        if key not in waits or waits[key][1] < val:
                            waits[key] = (sem, val)
                    for key, (sem, val) in waits.items():
                        eng.wait_ge(sem, val)
                        seen[key] = val
                    if op.meth is None:
                        continue
                    args = {k: (v.ap if isinstance(v, V) else v) for k, v in op.kw.items()}
                    inst = getattr(eng, op.meth)(**args)
                    if op.is_dma:
                        inst.then_inc(dsems[op.dsem], 16)
                    elif op.sig:
                        inst.then_inc(esems[e][(op.sigidx - 1) // EPOCH], 1)

            @block.tensor
            def _(eng):
                run("pe", eng)

            @block.scalar
            def _(eng):
                run("act", eng)

            @block.vector
            def _(eng):
                run("dve", eng)

            @block.gpsimd
            def _(eng):
                run("pool", eng)

            @block.sync
            def _(eng):
                run("sp", eng)


class Arena:
    def __init__(self, tensor, ncols):
        self.t = tensor
        self.n = ncols
        self.ptr = 0
        self.live = []
        self.retired = []

    def reset(self):
        self.retired = [x for x in self.retired] + self.live
        self.retired = self.retired[-400:]
        self.live = []
        self.ptr = 0

    def mark(self):
        return (self.ptr, len(self.live))

    def release(self, mk):
        ptr, nl = mk
        self.retired = (self.retired + self.live[nl:])[-400:]
        self.live = self.live[:nl]
        self.ptr = ptr

    def alloc(self, n, parts=128):
        a, b = self.ptr, self.ptr + n
        assert b <= self.n, f"arena overflow {b} > {self.n}"
        self.ptr = b
        r = Res()
        for (s, e, rr) in self.retired:
            if s < b and a < e:
                if rr.lw is not None:
                    r.rd.append(rr.lw)
                r.rd.extend(rr.rd)
        self.live.append((a, b, r))
        return V(self.t[0:parts, a:b], [r])


def build_program():
    nc = bass.Bass("TRN2", target_bir_lowering=False)
    P = Prog(nc)

    def din(name, shape):
        return nc.dram_tensor(name, list(shape), F32, kind="ExternalInput").ap()

    def dout(name, shape):
        return nc.dram_tensor(name, list(shape), F32, kind="ExternalOutput").ap()

    xp = din("xp", [SEQ, D])
    xs = din("xs", [NS, D])
    st_re = din("st_re", [NL, NSEQ, 4096])
    st_im = din("st_im", [NL, NSEQ, 4096])
    st_dl = din("st_dl", [NL, NSEQ, HC, 128, 128])
    st_cv = din("st_cv", [NL, NS - NSEQ, QKV])
    w_in = din("w_in", [NL, D, NIN])
    b_gate = din("b_gate", [NL, 48, 128])
    lam_re = din("lam_re", [NL, 32, 128])
    lam_im = din("lam_im", [NL, 32, 128])
    log_dt = din("log_dt", [NL, 32, 2])
    b_re = din("b_re", [NL, 4096, 16])
    b_im = din("b_im", [NL, 4096, 16])
    c_re = din("c_re", [NL, 1024, 64])
    c_im = din("c_im", [NL, 1024, 64])
    ssm_d = din("ssm_d", [NL, 8, 128])
    w_glu = din("w_glu", [NL, 1024, 1024])
    b_glu = din("b_glu", [NL, 8, 128])
    w_pa = din("w_pa", [NL, 1024, D])
    ln_v_g = din("ln_v_g", [NL, 1024])
    ln_v_b = din("ln_v_b", [NL, 1024])
    w_s = din("w_s", [NL, 8, 128, 128])
    b_s = din("b_s", [NL, 1024])
    w_pb = din("w_pb", [NL, 1024, D])
    conv_w = din("conv_w", [NL, 96, 128])
    a_log = din("a_log", [NL, 8])
    dt_bias = din("dt_bias", [NL, 8])
    norm_g = din("norm_g", [NL, 1, 128])
    w_pc = din("w_pc", [NL, 1024, D])
    w_out = din("w_out", [NL, D, D])
    ln_g = din("ln_g", [NL, 16, 128])
    ln_b = din("ln_b", [NL, 16, 128])

    y_p = dout("y_p", [SEQ, D])
    y_s = dout("y_s", [NS, D])
    sre_p = dout("sre_p", [NL, 32, 128])
    sim_p = dout("sim_p", [NL, 32, 128])
    dl_p = dout("dl_p", [NL, HC, 128, 128])
    cv_p = dout("cv_p", [NL, 3, QKV])
    sre_s = dout("sre_s", [NL, NSEQ, 4096])
    sim_s = dout("sim_s", [NL, NSEQ, 4096])
    dl_s = dout("dl_s", [NL, NSEQ, HC, 128, 128])
    cv_s = dout("cv_s", [NL, NSEQ, 3, QKV])
    gv_s = dout("gv_s", [NL, NS, 1024])

    st = contextlib.ExitStack()
    with st:
        def sbt(name, shape, dt=F32):
            return st.enter_context(nc.sbuf_tensor(name, list(shape), dt))

        def pst(name, shape):
            return st.enter_context(nc.psum_tensor(name, list(shape), F32))

        xT_t = sbt("xT", [128, 16, NT])
        mg_t = sbt("mg", [128, 16, NT])
        NWB = 3
        WBW = 2048
        wb_t = sbt("wbuf", [128, NWB, WBW], F32R)
        NWB16 = 2
        wb16_t = sbt("wbuf16", [128, NWB16, WBW], BF16)
        ACOLS = 12200
        ar_t = sbt("arena", [128, ACOLS])
        ar = Arena(ar_t, ACOLS)
        A16 = 9728
        ar16_t = sbt("arena16", [128, A16], BF16)
        ar16 = Arena(ar16_t, A16)
        cst_t = sbt("consts", [128, 7, 128])
        sm_t = sbt("small", [128, 640])
        gS_t = sbt("gdnS", [128, NL * HC, 128])
        s5p_t = sbt("s5p", [128, 4096])
        ws_t = sbt("wsT", [128, 8, 128], BF16)
        wss_t = sbt("wsTs", [64, 8, 64], BF16)
        misc_t = sbt("misc", [128, 1024])

        xT = [V(xT_t[:, k, :], [Res()]) for k in range(16)]
        mg = [V(mg_t[:, k, :], [Res()]) for k in range(16)]
        wbufs = [V(wb_t[:, i, :], [Res()]) for i in range(NWB)]
        wbufs16 = [V(wb16_t[:, i, :], [Res()]) for i in range(NWB16)]
        xTr = [x.r() for x in xT]
        ones = V(cst_t[:, 0, :], [Res()])
        ident = V(cst_t[:, 1, :], [Res()])
        mask_ui = V(cst_t[:, 2, :], [Res()])
        negmask_su = V(cst_t[:, 3, :], [Res()])
        krow = V(cst_t[:, 4, :], [Res()])
        bm64 = V(cst_t[0:64, 5, 0:64], [Res()])
        rep4 = V(cst_t[0:4, 6, 0:64], [Res()])
        negones = V(cst_t[:, 6, 64:128], [Res()])
        pmask = V(cst_t[:, 5, 64:68], [Res()])

        smo = [0]

        def sm_alloc(n):
            a = smo[0]
            smo[0] += n
            assert smo[0] <= 640
            return V(sm_t[:, a:a + n], [Res()])

        bgate_c = [sm_alloc(48) for _ in range(NL)]
        ssmd_c = [sm_alloc(8) for _ in range(NL)]
        bglu_c = [sm_alloc(8) for _ in range(NL)]
        lng_c = [sm_alloc(16) for _ in range(NL)]
        lnb_c = [sm_alloc(16) for _ in range(NL)]
        convw_c = [sm_alloc(96) for _ in range(NL)]
        normg_c = [sm_alloc(1) for _ in range(NL)]
        s5st_re = [sm_alloc(32) for _ in range(NL)]
        s5st_im = [sm_alloc(32) for _ in range(NL)]
        ctail = [[V(misc_t[:, (l * 24 + c) * 3:(l * 24 + c) * 3 + 3], [Res()]) for c in range(24)] for l in range(NL)]
        gS = [[V(gS_t[:, l * HC + h, :], [Res()]) for h in range(HC)] for l in range(NL)]
        mo = [NL * 24 * 3]

        def misc_alloc(n, parts=128):
            a = mo[0]
            mo[0] += n
            assert mo[0] <= 1024
            return V(misc_t[0:parts, a:a + n], [Res()])

        negea = [misc_alloc(8) for _ in range(NL)]
        dtb = [misc_alloc(8) for _ in range(NL)]

        banks = [pst(f"ps{i}", [128, 512]) for i in range(8)]
        bres = [Res(excl=True) for _ in range(8)]
        bankV = [V(banks[i][:, :], [bres[i]]) for i in range(8)]
        quarter = [[V(banks[i][:, q * 128:(q + 1) * 128], [bres[i]]) for q in range(4)] for i in range(8)]
        slot64 = [[V(banks[i][:, q * 64:(q + 1) * 64], [bres[i]]) for q in range(8)] for i in range(8)]

        wbi = [0]

        def wload(src2d, kc, cols):
            assert kc * cols <= WBW
            b = wbufs[wbi[0] % NWB]
            wbi[0] += 1
            dst = b[:, 0:kc * cols].re("p (k c) -> p k c", c=cols)
            P.dma("pool", out=dst, in_=src2d.rearrange("(k p) c -> p k c", p=128))
            return dst

        wbi16 = [0]

        def wload16(src2d, kc, cols):
            assert kc * cols <= WBW
            b = wbufs16[wbi16[0] % NWB16]
            wbi16[0] += 1
            dst = b[:, 0:kc * cols].re("p (k c) -> p k c", c=cols)
            P.dma("pool", out=dst, in_=src2d.rearrange("(k p) c -> p k c", p=128))
            return dst

        def act(out, in_, func, bias=0.0, scale=1.0, eng="act"):
            if isinstance(bias, float) and bias == 0.0:
                P.I("act", "activation", out=out, in_=in_, func=func, scale=scale)
            else:
                P.I("act", "activation", out=out, in_=in_, func=func, bias=bias, scale=scale)

        def tt(out, in0, in1, op, eng="dve"):
            P.I(eng, "tensor_tensor", out=out, in0=in0, in1=in1, op=op)

        def ts(out, in0, s1, op0, s2=None, op1=None, eng="dve"):
            if op1 is None:
                P.I(eng, "tensor_scalar", out=out, in0=in0, scalar1=s1, scalar2=None, op0=op0)
            else:
                P.I(eng, "tensor_scalar", out=out, in0=in0, scalar1=s1, scalar2=s2, op0=op0, op1=op1)

        def stt(out, in0, scalar, in1, op0, op1, eng="dve"):
            P.I(eng, "scalar_tensor_tensor", out=out, in0=in0, scalar=scalar, in1=in1, op0=op0, op1=op1)

        def cp(out, in_, eng="act"):
            if eng == "act":
                P.I("act", "copy", out=out, in_=in_)
            else:
                P.I(eng, "tensor_copy", out=out, in_=in_)

        def mm(out, lhsT, rhs, start=True, stop=True, tp=None):
            if tp is None:
                P.I("pe", "matmul", out=out, lhsT=lhsT, rhs=rhs, start=start, stop=stop)
            else:
                P.I("pe", "matmul", out=out, lhsT=lhsT, rhs=rhs, start=start, stop=stop, tile_position=tp)

        def tr(out, in_, n):
            P.I("pe", "transpose", out=out, in_=in_, identity=ident[0:n, 0:n])

        rr = {"main": 0, "s64": 0, "aux": 0}

        def ps_main():
            b = (0, 1)[rr["main"] % 2]
            rr["main"] += 1
            return bankV[b]

        def ps_s64():
            q = rr["s64"] % 8
            rr["s64"] += 1
            return slot64[2][q]

        def ps_aux():
            b = (3, 4)[rr["aux"] % 2]
            rr["aux"] += 1
            return bankV[b]

        def ps_for(gi, n):
            return ps_main()[:, 0:n] if gi == 0 else ps_s64()[:, 0:n]

        def proj_fm(wsrc, kc, ncols, rhs, groups, consume, getps=ps_for, loader=None, wview=None):
            cg = WBW // (kc * 128)
            cg = min(cg, ncols // 128)
            for c0 in range(0, ncols, cg * 128):
                w = (loader or wload)(wsrc[:, c0:c0 + cg * 128], kc, cg * 128)
                if wview is not None:
                    w = wview(w)
                for mi in range(cg):
                    m = c0 // 128 + mi
                    for gi, (col0, n) in enumerate(groups):
                        ps = getps(gi, n)
                        for k in range(kc):
                            mm(ps, w[:, k, mi * 128:(mi + 1) * 128], rhs[k][:, col0:col0 + n], start=(k == 0), stop=(k == kc - 1))
                        consume(m, gi, ps, col0, n)

        def load_cols(dst, src2d, n):
            stg = ar.alloc(128, parts=n)
            P.dma("sp", out=stg, in_=src2d)
            ps = quarter[7][3][:, 0:n]
            tr(ps, stg, n)
            cp(dst, ps, eng="dve")

        P.I("pool", "memset", ap=ones, constant=1.0)
        P.I("pool", "memset", ap=negones, constant=-1.0)
        P.I("pool", "affine_select", out=ident, in_=ones, pattern=[[-1, 128]], compare_op=ALU.is_equal, fill=0.0, base=0, channel_multiplier=1)
        P.I("pool", "affine_select", out=mask_ui, in_=ones, pattern=[[1, 128]], compare_op=ALU.is_ge, fill=0.0, base=0, channel_multiplier=-1)
        P.I("pool", "affine_select", out=negmask_su[:, 0:64], in_=negones, pattern=[[1, 64]], compare_op=ALU.is_ge, fill=0.0, base=-1, channel_multiplier=-1)
        P.I("pool", "affine_select", out=negmask_su[:, 64:128], in_=negones, pattern=[[1, 64]], compare_op=ALU.is_ge, fill=0.0, base=63, channel_multiplier=-1)
        P.I("pool", "iota", out=krow, pattern=[[1, 128]], base=1, channel_multiplier=0, allow_small_or_imprecise_dtypes=True)
        for q_ in range(4):
            P.I("pool", "affine_select", out=pmask[:, q_:q_ + 1], in_=ones[:, 0:1], pattern=[[0, 1]], compare_op=ALU.is_ge, fill=0.0, base=-32 * q_, channel_multiplier=1)
            P.I("pool", "affine_select", out=pmask[:, q_:q_ + 1], in_=pmask[:, q_:q_ + 1], pattern=[[0, 1]], compare_op=ALU.is_ge, fill=0.0, base=32 * q_ + 31, channel_multiplier=-1)
        cp(bm64, mask_ui[0:64, 0:64], eng="pool")
        for a in range(1, 16):
            P.I("pool", "affine_select", out=bm64[:, 4 * a:4 * a + 4], in_=bm64[:, 4 * a:4 * a + 4], pattern=[[0, 4]], compare_op=ALU.is_ge, fill=0.0, base=-4 * a, channel_multiplier=1)
        cp(rep4.re("p (b j) -> p b j", j=4), ident[0:4, 0:4].un(1).bc([4, 16, 4]), eng="pool")
        for l in range(NL):
            for k in range(24):
                P.I("pool", "memset", ap=ctail[l][k], constant=0.0)
            for h in range(HC):
                P.I("pool", "memset", ap=gS[l][h], constant=0.0)
            P.I("pool", "memset", ap=s5st_re[l], constant=0.0)
            P.I("pool", "memset", ap=s5st_im[l], constant=0.0)

        ar.reset()
        for l in range(NL):
            load_cols(bgate_c[l], b_gate[l], 48)
            load_cols(ssmd_c[l], ssm_d[l], 8)
            load_cols(bglu_c[l], b_glu[l], 8)
            load_cols(lng_c[l], ln_g[l], 16)
            load_cols(lnb_c[l], ln_b[l], 16)
            load_cols(convw_c[l], conv_w[l], 96)
            load_cols(normg_c[l], norm_g[l], 1)
            P.dma("sp", out=negea[l], in_=a_log[l].partition_broadcast(128))
            act(negea[l], negea[l], AF.Exp)
            ts(negea[l], negea[l], -1.0, ALU.mult)
            P.dma("sp", out=dtb[l], in_=dt_bias[l].partition_broadcast(128))

        def load_x(blk, groups):
            ar.reset()
            tiles = [(xp[blk * BLK + i * 128: blk * BLK + (i + 1) * 128, :], i * 128, 128) for i in range(4)]
            if blk == 0:
                tiles.append((xs, BLK, NS))
            stg = [ar.alloc(D), ar.alloc(D)]
            for ti, (src, col0, n) in enumerate(tiles):
                s = stg[ti % 2]
                P.dma("sp", out=s[0:n, :], in_=src)
                for k4 in range(4):
                    ps = ps_main()
                    for j in range(4):
                        k = k4 * 4 + j
                        tr(ps[:, j * 128:j * 128 + n], s[0:n, k * 128:(k + 1) * 128], n)
                    for j in range(4):
                        k = k4 * 4 + j
                        cp(xT[k][:, col0:col0 + n].r(), ps[:, j * 128:j * 128 + n], eng="dve")

        def s5_prep(l):
            o = [0]

            def pal(n):
                a = o[0]
                o[0] += n
                assert o[0] <= 4096
                return V(s5p_t[:, a:a + n], [Res()])

            BTre = pal(1024).re("p (c s) -> p c s", s=128)
            BTim = pal(1024).re("p (c s) -> p c s", s=128)
            CTre = pal(1024).re("p (q c) -> p q c", c=32)
            CTim = pal(1024).re("p (q c) -> p q c", c=32)
            mag, th, abr, abi = [ar.alloc(32) for _ in range(4)]
            mk_ = ar.mark()
            lr, li, dt_, fr, fi, t0, t1, t2 = [ar.alloc(32) for _ in range(8)]
            load_cols(lr, lam_re[l], 32)
            load_cols(li, lam_im[l], 32)
            ldt = ar.alloc(2, parts=32)
            P.dma("sp", out=ldt, in_=log_dt[l])
            ldtb = ar.alloc(128, parts=32)
            cp(ldtb.re("p (g s) -> p g s", s=64), ldt.un(2).bc([32, 2, 64]), eng="dve")
            psq = quarter[7][2][:, 0:32]
            tr(psq, ldtb, 32)
            act(dt_, psq, AF.Exp)
            tt(t0, lr, dt_, ALU.mult)
            act(mag, t0, AF.Exp)
            tt(th, li, dt_, ALU.mult)

            def rsin(out, ang, shift):
                a2 = ar.alloc(ang.ap.shape[1])
                k = ar.alloc(ang.ap.shape[1])
                if shift != 0.0:
                    ts(a2, ang, shift, ALU.add)
                else:
                    cp(a2, ang, eng="dve")
                ts(k, a2, 1.0 / TWO_PI, ALU.mult, MAGIC, ALU.add)
                ts(k, k, MAGIC, ALU.subtract, TWO_PI, ALU.mult)
                tt(a2, a2, k, ALU.subtract)
                act(out, a2, AF.Sin)

            rsin(t1, th, 0.0)
            rsin(t2, th, math.pi / 2)
            tt(abr, mag, t2, ALU.mult)
            tt(abi, mag, t1, ALU.mult)
            den, nr, u0, u1 = [ar.alloc(32) for _ in range(4)]
            tt(den, lr, lr, ALU.mult)
            tt(u0, li, li, ALU.mult)
            tt(den, den, u0, ALU.add)
            P.I("dve", "reciprocal", out=den, in_=den)
            ts(nr, abr, -1.0, ALU.add)
            tt(u0, nr, lr, ALU.mult)
            tt(u1, abi, li, ALU.mult)
            tt(u0, u0, u1, ALU.add)
            tt(fr, u0, den, ALU.mult)
            tt(u0, abi, lr, ALU.mult)
            tt(u1, nr, li, ALU.mult)
            tt(u0, u0, u1, ALU.subtract)
            tt(fi, u0, den, ALU.mult)
            import os
            ks5 = float(os.environ.get("KS5", "9"))
            if ks5 <= 1:
                return None
            Br = ar.alloc(512).re("p (q c) -> p q c", c=16)
            Bi = ar.alloc(512).re("p (q c) -> p q c", c=16)
            for i4 in range(4):
                qs = slice(8 * i4, 8 * i4 + 8)
                P.dma("sp", out=Br[:, qs, :], in_=b_re[l].rearrange("(q s) c -> s q c", s=128)[:, qs, :])
                P.dma("sp", out=Bi[:, qs, :], in_=b_im[l].rearrange("(q s) c -> s q c", s=128)[:, qs, :])
            if ks5 <= 1.1:
                return None
            frb = fr.un(2).bc([128, 32, 16])
            fib = fi.un(2).bc([128, 32, 16])
            w0 = ar.alloc(512).re("p (q c) -> p q c", c=16)
            w1 = ar.alloc(512).re("p (q c) -> p q c", c=16)
            INr = ar.alloc(1024).re("p (q c) -> p q c", c=32)
            INi = ar.alloc(1024).re("p (q c) -> p q c", c=32)
            P.I("dve", "memset", ap=INr, constant=0.0)
            P.I("dve", "memset", ap=INi, constant=0.0)
            if ks5 <= 1.2:
                return None
            tt(w0, Br, frb, ALU.mult)
            tt(w1, Bi, fib, ALU.mult)
            tt(INr[0:64, :, 0:16], w0[0:64], w1[0:64], ALU.subtract)
            tt(INr[64:128, :, 16:32], w0[64:128], w1[64:128], ALU.subtract)
            tt(w0, Bi, frb, ALU.mult)
            tt(w1, Br, fib, ALU.mult)
            tt(INi[0:64, :, 0:16], w0[0:64], w1[0:64], ALU.add)
            tt(INi[64:128, :, 16:32], w0[64:128], w1[64:128], ALU.add)
            if ks5 <= 1.5:
                return None
            kc_ = int(os.environ.get("KC", "8"))
            kv_ = os.environ.get("KV", "")
            for c in range(kc_):
                for (IN, BT, qd) in ((INr, BTre, 0), (INi, BTim, 1)):
                    ps = quarter[7][qd]
                    if kv_ == "notr_id":
                        tr(ps, ones, 128)
                        continue
                    tr(ps, IN[:, 4 * c:4 * c + 4, :].re("p q c -> p (q c)"), 128)
                    if kv_ == "notr":
                        continue
                    cp(BT[:, c, :], ps, eng=("dve" if kv_ == "dve" else ("act" if qd == 0 else "dve")))
            if ks5 <= 2:
                return None
            for (csrc, CT, sign) in ((c_re, CTre, 1.0), (c_im, CTim, -1.0)):
                Cl = ar.alloc(1024).re("p (c d s) -> p c d s", d=2, s=64)
                P.dma("sp", out=Cl[:, :, 0, :], in_=csrc[l].rearrange("(c r) s -> r c s", r=128))
                P.dma("sp", out=Cl[:, :, 1, :], in_=csrc[l].rearrange("(c r) s -> r c s", r=128))
                P.I("dve", "memset", ap=CT, constant=0.0)
                for c in range(8):
                    ps = quarter[7][c % 2]
                    tr(ps, Cl[:, c, :, :].re("p d s -> p (d s)"), 128)
                    psv = ps.re("p (q g c) -> p q g c", g=2, c=16)
                    if sign > 0:
                        cp(CT[0:64, 4 * c:4 * c + 4, 0:16], psv[0:64, :, 0, :], eng="act")
                        cp(CT[64:128, 4 * c:4 * c + 4, 16:32], psv[64:128, :, 1, :], eng="dve")
                    else:
                        ts(CT[0:64, 4 * c:4 * c + 4, 0:16], psv[0:64, :, 0, :], -1.0, ALU.mult)
                        ts(CT[64:128, 4 * c:4 * c + 4, 16:32], psv[64:128, :, 1, :], -1.0, ALU.mult)
            ar.release(mk_)
            return dict(BTre=BTre, BTim=BTim, CTre=CTre, CTim=CTim, mag=mag, th=th, abr=abr, abi=abi)

        def merge_branch(l, br, wp, yT, groups):
            def getps(gi, n):
                return ps_main()[:, 0:n] if gi == 0 else ps_s64()[:, 0:n]

            for m in range(16):
                if m % 2 == 0:
                    wP2 = wload16(wp[:, m * 128:(m + 2) * 128], 8, 256)
                wP = wP2[:, :, (m % 2) * 128:(m % 2) * 128 + 128]
                zc = O_ZG + br * D + m * 128
                wZ = wload(w_in[l][:, zc:zc + 128], 16, 128)
                for gi, (col0, n) in enumerate(groups):
                    pp = getps(gi, n)
                    for k in range(8):
                        mm(pp, wP[:, k, :], yT[k][:, col0:col0 + n], start=(k == 0), stop=(k == 7))
                    pz = (ps_aux()[:, 0:n] if gi == 0 else ps_s64()[:, 0:n])
                    for k in range(16):
                        mm(pz, wZ[:, k, :], xT[k][:, col0:col0 + n].r(), start=(k == 0), stop=(k == 15))
                    g = mtmp[gi % 2][:, 0:n]
                    act(g, pz, AF.Sigmoid, bias=bgate_c[l][:, br * 16 + m:br * 16 + m + 1])
                    if br == 0:
                        tt(mg[m][:, col0:col0 + n].r(), g, pp, ALU.mult)
                    else:
                        tt(g, g, pp, ALU.mult)
                        tt(mg[m][:, col0:col0 + n].r(), mg[m][:, col0:col0 + n].f(), g, ALU.add)

        def phase_A(l, blk, groups):
            ar.reset()
            ar16.reset()
            global mtmp
            import os
            ksub = int(os.environ.get("KSUB", "99"))
            s5 = s5_prep(l)
            if ksub <= 0:
                return
            uT = [ar.alloc(NT) for _ in range(8)]
            ya = [ar16.alloc(NT) for _ in range(8)]

            def cons_u(m, gi, ps, c0, n):
                cp(uT[m][:, c0:c0 + n], ps, eng="act")

            proj_fm(w_in[l][:, O_XA:O_XA + 1024], 16, 1024, xTr, groups, cons_u)
            if ksub <= 1:
                return
            cosT, sinT, ang, rk, t1, t2, t3, s_re, s_im, p2s = [ar.alloc(BLK) for _ in range(10)]
            ytmp_s = ar.alloc(NS)
            um = [ar.alloc(NT), ar.alloc(NT)]
            has_s = len(groups) > 1
            if has_s:
                sst_re = [ar.alloc(128, parts=NSEQ)] * 2
                sst_im = [ar.alloc(128, parts=NSEQ)] * 2
                so_re = [ar.alloc(128, parts=NSEQ)] * 2
                so_im = [ar.alloc(128, parts=NSEQ)] * 2
                ss_re = [ar.alloc(NSEQ), ar.alloc(NS)]
                ss_im = [ar.alloc(NSEQ), ar.alloc(NS)]
                stm = [ar.alloc(NSEQ) for _ in range(2)]
            v3 = lambda v: v.re("p (n k) -> p n k", k=128)
            for c in range(8 if ksub > 2 else 1):
                ypb = bankV[6]
                for ql in range(4):
                    q = 4 * c + ql
                    rs = slice(32 * ql, 32 * ql + 32)
                    ts(ang, krow512, s5["th"][:, q:q + 1], ALU.mult)
                    ts(rk, ang, 1.0 / TWO_PI, ALU.mult, MAGIC, ALU.add)
                    ts(rk, rk, MAGIC, ALU.subtract, TWO_PI, ALU.mult)
                    tt(ang, ang, rk, ALU.subtract)
                    act(sinT, ang, AF.Sin)
                    act(rk, ang, AF.Abs)
                    act(cosT, rk, AF.Sin, bias=halfpi, scale=-1.0)
                    pbr, pbi = bankV[3], bankV[4]
                    umq = um[q % 2]
                    ncol = NT if has_s else BLK
                    P.I("act", "activation", out=umq[:, 0:ncol], in_=uT[c][:, 0:ncol], func=AF.Copy, scale=pmask[:, ql:ql + 1])
                    mm(pbr, s5["BTre"][:, c, :], umq[:, 0:BLK])
                    mm(pbi, s5["BTim"][:, c, :], umq[:, 0:BLK])
                    tt(t1, pbr, cosT, ALU.mult)
                    tt(t2, pbi, sinT, ALU.mult)
                    tt(t1, t1, t2, ALU.add)
                    tt(t3, pbi, cosT, ALU.mult)
                    tt(t2, pbr, sinT, ALU.mult)
                    tt(t3, t3, t2, ALU.subtract)
                    magb = s5["mag"][:, q:q + 1].bc([128, BLK])
                    P.I("dve", "tensor_tensor_scan", out=t1, data0=magb, data1=t1, initial=s5st_re[l][:, q:q + 1], op0=ALU.mult, op1=ALU.add)
                    P.I("dve", "tensor_tensor_scan", out=t3, data0=magb, data1=t3, initial=s5st_im[l][:, q:q + 1], op0=ALU.mult, op1=ALU.add)
                    tt(t2, t3, sinT, ALU.mult)
                    tt(s_re, t1, cosT, ALU.mult)
                    tt(s_re, s_re, t2, ALU.subtract)
                    tt(p2s, t3, cosT, ALU.mult, eng="pool")
                    tt(s_im, t1, sinT, ALU.mult, eng="pool")
                    tt(s_im, s_im, p2s, ALU.add, eng="pool")
                    cp(s5st_re[l][:, q:q + 1], s_re[:, BLK - 1:BLK], eng="act")
                    cp(s5st_im[l][:, q:q + 1], s_im[:, BLK - 1:BLK], eng="act")
                    mm(ypb[rs, :], s5["CTre"][:, q, :], s_re, start=True, stop=False, tp=(0, 32 * ql))
                    mm(ypb[rs, :], s5["CTim"][:, q, :], s_im, start=False, stop=True, tp=(0, 32 * ql))
                    if has_s:
                        psr, psi = slot64[5][0], slot64[5][1]
                        mm(psr, s5["BTre"][:, c, :], umq[:, BLK:NT])
                        mm(psi, s5["BTim"][:, c, :], umq[:, BLK:NT])
                        ptr_, pti_ = slot64[5][2][:, 0:NSEQ], slot64[5][3][:, 0:NSEQ]
                        P.dma("sp", out=sst_re[q % 2], in_=st_re[l][:, q * 128:(q + 1) * 128])
                        P.dma("sp", out=sst_im[q % 2], in_=st_im[l][:, q * 128:(q + 1) * 128])
                        tr(ptr_, sst_re[q % 2], NSEQ)
                        tr(pti_, sst_im[q % 2], NSEQ)
                        cp(ss_re[0], ptr_, eng="act")
                        cp(ss_im[0], pti_, eng="act")
                        arc, aic = s5["abr"][:, q:q + 1], s5["abi"][:, q:q + 1]
                        sr3 = ss_re[1].re("p (s t) -> p s t", t=4)
                        si3 = ss_im[1].re("p (s t) -> p s t", t=4)
                        br3 = psr.re("p (s t) -> p s t", t=4)
                        bi3 = psi.re("p (s t) -> p s t", t=4)
                        for t in range(4):
                            pr = ss_re[0] if t == 0 else sr3[:, :, t - 1]
                            pi_ = ss_im[0] if t == 0 else si3[:, :, t - 1]
                            ts(stm[0], pi_, aic, ALU.mult)
                            stt(stm[0], pr, arc, stm[0], ALU.mult, ALU.subtract)
                            ts(stm[1], pr, aic, ALU.mult)
                            stt(stm[1], pi_, arc, stm[1], ALU.mult, ALU.add)
                            tt(sr3[:, :, t], stm[0], br3[:, :, t], ALU.add)
                            tt(si3[:, :, t], stm[1], bi3[:, :, t], ALU.add)
                        yps = slot64[7][0]
                        mm(yps[rs, :], s5["CTre"][:, q, :], ss_re[1], start=True, stop=False, tp=(0, 32 * ql))
                        mm(yps[rs, :], s5["CTim"][:, q, :], ss_im[1], start=False, stop=True, tp=(0, 32 * ql))
                        pfr, pfi = quarter[7][1][0:NSEQ, :], quarter[7][2][0:NSEQ, :]
                        tr(pfr, sr3[:, :, 3], 128)
                        tr(pfi, si3[:, :, 3], 128)
                        cp(so_re[q % 2], pfr, eng="act")
                        cp(so_im[q % 2], pfi, eng="act")
                        P.dma("sp", is_out=True, out=sre_s[l][:, q * 128:(q + 1) * 128], in_=so_re[q % 2])
                        P.dma("sp", is_out=True, out=sim_s[l][:, q * 128:(q + 1) * 128], in_=so_im[q % 2])
                stt(t2, uT[c][:, 0:BLK], ssmd_c[l][:, c:c + 1], bankV[6], ALU.mult, ALU.add)
                act(uT[c][:, 0:BLK], t2, AF.Gelu)
                if has_s:
                    stt(ytmp_s, uT[c][:, BLK:NT], ssmd_c[l][:, c:c + 1], slot64[7][0], ALU.mult, ALU.add)
                    act(uT[c][:, BLK:NT], ytmp_s, AF.Gelu)
            if blk == NBLK - 1:
                for (stc, dst) in ((s5st_re[l], sre_p[l]), (s5st_im[l], sim_p[l])):
                    pf = quarter[7][0][0:32, :]
                    tr(pf, stc, 128)
                    fo = ar.alloc(128, parts=32)
                    cp(fo, pf, eng="act")
                    P.dma("sp", is_out=True, out=dst, in_=fo)
            if ksub <= 3:
                return
            zT = uT
            gtmp = [t1, t3]
            stmp = [s_re, s_im]
            for m in range(8):
                wGl = wload(w_glu[l][:, m * 128:(m + 1) * 128], 8, 128).f()
                wGa = wload(w_in[l][:, O_GA + m * 128:O_GA + (m + 1) * 128], 16, 128)
                for gi, (col0, n) in enumerate(groups):
                    p1 = ps_for(gi, n)
                    for k in range(8):
                        mm(p1, wGl[:, k, :], zT[k][:, col0:col0 + n], start=(k == 0), stop=(k == 7))
                    p2 = (ps_aux()[:, 0:n] if gi == 0 else ps_s64()[:, 0:n])
                    for k in range(16):
                        mm(p2, wGa[:, k, :], xTr[k][:, col0:col0 + n], start=(k == 0), stop=(k == 15))
                    g = gtmp[gi][:, 0:n]
                    sg = stmp[gi][:, 0:n]
                    act(g, p1, AF.Sigmoid, bias=bglu_c[l][:, m:m + 1])
                    act(sg, p2, AF.Silu)
                    tt(g, g, zT[m][:, col0:col0 + n], ALU.mult)
                    tt(ya[m][:, col0:col0 + n], g, sg, ALU.mult)
            mtmp = [t1, t3]
            if ksub <= 4:
                return
            merge_branch(l, 0, w_pa[l], ya, groups)

        def phase_B(l, blk, groups):
            ar.reset()
            ar16.reset()
            global mtmp
            ybT = [ar16.alloc(NT) for _ in range(8)]
            has_s = len(groups) > 1
            ntile = 5 if has_s else 4
            vn16 = [ar16.alloc(1024) for _ in range(ntile)]
            vtok = [ar.alloc(1024) for _ in range(ntile)]
            bsb = ar.alloc(1024)
            mkB = ar.mark()
            lnvg = ar.alloc(1024)
            lnvb = ar.alloc(1024)
            P.dma("sp", out=lnvg, in_=ln_v_g[l].partition_broadcast(128))
            P.dma("sp", out=lnvb, in_=ln_v_b[l].partition_broadcast(128))
            P.dma("sp", out=bsb, in_=b_s[l].partition_broadcast(128))
            vf = [ar.alloc(BLK), ar.alloc(BLK)]
            cnt = [0]

            def cons_v(m, gi, ps, c0, n):
                t = vf[cnt[0] % 2]
                cnt[0] += 1
                cp(t[:, 0:n], ps, eng="act")
                if gi == 0:
                    pq = ps_aux()
                    for i in range(4):
                        tr(pq[:, i * 128:(i + 1) * 128], t[:, i * 128:(i + 1) * 128], 128)
                    for i in range(4):
                        cp(vtok[i][:, m * 128:(m + 1) * 128], pq[:, i * 128:(i + 1) * 128], eng="dve")
                else:
                    pq = quarter[7][m % 4][0:NS, :]
                    tr(pq, t[:, 0:NS], 128)
                    cp(vtok[4][0:NS, m * 128:(m + 1) * 128], pq, eng="dve")

            proj_fm(w_in[l][:, O_VB:O_VB + 1024], 16, 1024, xTr, groups, cons_v)
            stats = ar.alloc(12)
            mv = ar.alloc(2)
            for i in range(ntile):
                n = 128 if i < 4 else NS
                v = vtok[i][0:n, :]
                st_ = stats[0:n, :].re("p (a b) -> p a b", b=6)
                P.I("dve", "bn_stats", out=st_[:, 0, :], in_=v[:, 0:512])
                P.I("dve", "bn_stats", out=st_[:, 1, :], in_=v[:, 512:1024])
                P.I("dve", "bn_aggr", out=mv[0:n, :], in_=st_)
                act(mv[0:n, 1:2], mv[0:n, 1:2], AF.Sqrt, bias=eps_ln[0:n, :])
                P.I("dve", "reciprocal", out=mv[0:n, 1:2], in_=mv[0:n, 1:2])
                ts(v, v, mv[0:n, 0:1], ALU.subtract, mv[0:n, 1:2], ALU.mult)
                tt(v, v, lnvg[0:n, :], ALU.mult)
                if i < 4:
                    tt(vn16[i][0:n, :], v, lnvb[0:n, :], ALU.add)
                else:
                    tt(v, v, lnvb[0:n, :], ALU.add)
                    P.dma("sp", is_out=True, out=gv_s[l], in_=v)
                    cp(vn16[i][0:n, :], v, eng="dve")
            ar.release(mkB)
            wst = ar.alloc(128)
            wsT = [V(ws_t[:, g, :], [Res()]) for g in range(8)] if not ws_cache else ws_cache[0]
            wsTs = [V(wss_t[:, g, :], [Res()]) for g in range(8)] if not ws_cache else ws_cache[1]
            if not ws_cache:
                ws_cache.extend([wsT, wsTs])
            y4 = ar.alloc(64, parts=4)
            wtmp = ar.alloc(128)
            for g in range(8):
                P.dma("sp", out=wst, in_=w_s[l, g])
                pq = quarter[7][g % 4]
                tr(pq, wst, 128)
                tt(wtmp, pq, mask_ui, ALU.mult)
                cp(wsT[g], wtmp, eng="dve")
                if has_s:
                    cp(y4.re("p (b i) -> p b i", i=4), wtmp[0:4, 0:4].un(1).bc([4, 16, 4]), eng="dve")
                    p64 = slot64[7][4 + g % 4][0:64, :]
                    mm(p64, rep4, y4)
                    tt(wsTs[g], p64, bm64, ALU.mult)
            tA = [ar.alloc(BLK), ar.alloc(BLK)]
            tB = [ar.alloc(BLK), ar.alloc(BLK)]
            for g in range(8):
                gc_ = slice(g * 128, (g + 1) * 128)
                wU = wload(w_in[l][:, O_UB + g * 128:O_UB + (g + 1) * 128], 16, 128)
                wG = wload(w_in[l][:, O_GB + g * 128:O_GB + (g + 1) * 128], 16, 128)
                for gi, (col0, n) in enumerate(groups):
                    if gi == 0:
                        sps = bankV[5]
                        for i in range(4):
                            mm(sps[:, i * 128:(i + 1) * 128], vn16[i][:, gc_], wsT[g])
                        pu, pg = ps_main(), ps_aux()
                        bias_v = bsb[:, gc_].un(1).bc([128, 4, 128])
                        v3 = lambda v: v.re("p (n k) -> p n k", k=128)
                    else:
                        sps = slot64[6][g % 8]
                        mm(sps, vn16[4][0:NS, gc_], wsTs[g])
                        pu, pg = ps_s64(), ps_s64()
                        bias_v = bsb[:, g * 128:g * 128 + 4].un(1).bc([128, 16, 4])
                        v3 = lambda v: v.re("p (s t) -> p s t", t=4)
                    for k in range(16):
                        mm(pu[:, 0:n], wU[:, k, :], xTr[k][:, col0:col0 + n], start=(k == 0), stop=(k == 15))
                    for k in range(16):
                        mm(pg[:, 0:n], wG[:, k, :], xTr[k][:, col0:col0 + n], start=(k == 0), stop=(k == 15))
                    a_, b_ = tA[gi][:, 0:n], tB[gi][:, 0:n]
                    act(a_, pg[:, 0:n], AF.Silu)
                    tt(v3(b_), v3(sps[:, 0:n]), bias_v, ALU.add)
                    tt(b_, b_, pu[:, 0:n], ALU.mult)
                    tt(ybT[g][:, col0:col0 + n], b_, a_, ALU.mult)
            mtmp = [ar.alloc(BLK), ar.alloc(BLK)]
            merge_branch(l, 1, w_pb[l], ybT, groups)

        def gdn_items(items, slots):
            n_it = len(items)
            for i, it in enumerate(items):
                it["s"] = slots[i % len(slots)]
            for it in items:
                T, s = it["T"], it["s"]
                pq = s["pq8"][0:4]
                gmat, bmat = s["t0"][0:T, :], s["t1"][0:T, :]
                ts(gmat, ones[0:T, :], it["gcol"], ALU.mult)
                ts(bmat[:, 0:T], ones[0:T, 0:T], it["bcol"], ALU.mult)
                mm(pq[0][0:T, 0:1], mask_ui[0:T, 0:T], it["gcol"])
                mm(pq[1][:, 0:T], gmat, mask_ui[0:T, 0:T])
                mm(pq[2][0:T, 0:T], bmat[:, 0:T], ident[0:T, 0:T])
                mm(pq[3][0:T, 0:T], it["kT"], it["kT"])
                cp(s["gccol"][0:T, :], pq[0][0:T, 0:1], eng="act")
                cp(s["gcrow"][:, 0:T], pq[1][:, 0:T], eng="act")
            for it in items:
                T, s = it["T"], it["s"]
                pq = s["pq8"][0:4]
                dT = s["t0"][0:T, 0:T]
                ts(dT, s["gcrow"][0:T, 0:T], s["gccol"][0:T, :], ALU.subtract, 0.0, ALU.min)
                act(dT, dT, AF.Exp)
                tt(s["decm"][0:T, 0:T], dT, mask_ui[0:T, 0:T], ALU.mult)
                Y = s["Y"][0:T, 0:T]
                tt(Y, dT, negmask_su[0:T, 0:T], ALU.mult)
                tt(Y, Y, pq[2][0:T, 0:T], ALU.mult)
                tt(Y, Y, pq[3][0:T, 0:T], ALU.mult)
                sc = s["sc"]
                act(sc[0:T, 0:1], s["gccol"][0:T, :], AF.Exp)
                tt(sc[0:T, 1:2], sc[0:T, 0:1], it["bcol"], ALU.mult)
                act(sc[0:T, 2:3], s["gccol"][0:T, :], AF.Exp, bias=s["gcrow"][0:T, T - 1:T], scale=-1.0)
                act(sc[:, 3:4], s["gcrow"][:, T - 1:T], AF.Exp)
                act(s["egrow"][:, 0:T], s["gcrow"][:, 0:T], AF.Exp)
            for it in items:
                T, s = it["T"], it["s"]
                pq = s["pq8"][4:8]
                tr(pq[0][0:T, 0:T], s["Y"][0:T, 0:T], T)
                cp(s["A"][0:T, 0:T], pq[0][0:T, 0:T], eng="act")
                tt(s["MT"][0:T, 0:T], s["Y"][0:T, 0:T], ident[0:T, 0:T], ALU.add)
                it["X"], it["XT"] = s["Y"], s["A"]
                it["pp"] = 0
            nsteps = max(int(math.ceil(math.log2(it["T"]))) - 1 for it in items)
            for stp in range(nsteps):
                for it in items:
                    T, s = it["T"], it["s"]
                    if stp >= int(math.ceil(math.log2(T))) - 1:
                        continue
                    pq = s["pq8"][(stp % 2) * 4:(stp % 2) * 4 + 4]
                    X, XT = it["X"][0:T, 0:T], it["XT"][0:T, 0:T]
                    X2s, XT2s = (s["X1"], s["XT1"]) if it["pp"] == 0 else (s["X0"], s["XT0"])
                    it["pp"] ^= 1
                    mm(pq[1][0:T, 0:T], XT, X)
                    mm(pq[2][0:T, 0:T], X, XT)
                    cp(X2s[0:T, 0:T], pq[1][0:T, 0:T], eng="act")
                    cp(XT2s[0:T, 0:T], pq[2][0:T, 0:T], eng="dve")
                    it["X"], it["XT"] = X2s, XT2s
                for it in items:
                    T, s = it["T"], it["s"]
                    if stp >= int(math.ceil(math.log2(T))) - 1:
                        continue
                    pq = s["pq8"][(stp % 2) * 4:(stp % 2) * 4 + 4]
                    mm(pq[3][0:T, 0:T], it["XT"][0:T, 0:T], s["MT"][0:T, 0:T])
                    tt(s["MT"][0:T, 0:T], s["MT"][0:T, 0:T], pq[3][0:T, 0:T], ALU.add)
            for it in items:
                T, s = it["T"], it["s"]
                pq = s["pq8"][0:4]
                sc = s["sc"]
                vb_, kbg, kdec = s["t0"][0:T, :], s["t1"][0:T, :], s["kdec"][0:T, :]
                ts(vb_, it["vtok"], it["bcol"], ALU.mult)
                ts(kbg, it["ktok"], sc[0:T, 1:2], ALU.mult)
                ts(kdec, it["ktok"], sc[0:T, 2:3], ALU.mult)
                MT = s["MT"][0:T, 0:T]
                mm(pq[0][0:T, :], MT, vb_)
                mm(pq[1][:, 0:T], kbg, MT)
                mm(pq[2][0:T, 0:T], it["kT"], it["qT"])
                cp(s["X0"][0:T, :], pq[0][0:T, :], eng="act")
                cp(s["XT0"][:, 0:T], pq[1][:, 0:T], eng="act")
                tt(s["X1"][0:T, 0:T], pq[2][0:T, 0:T], s["decm"][0:T, 0:T], ALU.mult)
                tt(s["XT1"][:, 0:T], it["qT"], s["egrow"][:, 0:T], ALU.mult)
            for it in items:
                T, s = it["T"], it["s"]
                pq = s["pq8"][4:8]
                sc = s["sc"]
                S = it["S"]
                mm(pq[0][0:T, :], s["XT0"][:, 0:T], S)
                vnew = s["Y"][0:T, :]
                tt(vnew, s["X0"][0:T, :], pq[0][0:T, :], ALU.subtract)
                mm(pq[1][:, 0:T], S, s["XT1"][:, 0:T], start=True, stop=False)
                mm(pq[1][:, 0:T], vnew, s["X1"][0:T, 0:T], start=False, stop=True)
                mm(pq[2], s["kdec"][0:T, :], vnew)
                stt(S, S, sc[:, 3:4], pq[2], ALU.mult, ALU.add)
            for it in items:
                T, s = it["T"], it["s"]
                pq = s["pq8"][4:8]
                sq = s["A"][:, 0:T]
                act(sq, pq[1][:, 0:T], AF.Square)
                mm(pq[3][:, 0:T], ones, sq)
            for it in items:
                T, s = it["T"], it["s"]
                pq = s["pq8"][4:8]
                rstd = s["A"][:, 0:T]
                act(rstd, pq[3][:, 0:T], AF.Ln, bias=eps_n, scale=1.0 / 128.0)
                act(rstd, rstd, AF.Exp, scale=-0.5)
                tt(rstd, rstd, pq[1][:, 0:T], ALU.mult)
                stt(it["odst"], rstd, it["ncol"], it["sg"], ALU.mult, ALU.mult)

        def make_slots(nslots, bank0):
            sl = []
            for i in range(nslots):
                d = {}
                for nm in ("t0", "t1", "gcrow", "egrow", "decm", "Y", "A", "X0", "XT0", "X1", "XT1", "MT", "kdec"):
                    d[nm] = ar.alloc(128)
                d["gccol"] = ar.alloc(1)
                d["sc"] = ar.alloc(4)
                d["pq8"] = [quarter[bank0 + i][qq] for qq in range(4)] * 2
                d["ri"] = [0]
                sl.append(d)
            return sl

        def phase_C(l, blk, groups):
            ar.reset()
            ar16.reset()
            global mtmp
            has_s = len(groups) > 1
            ycT = [ar16.alloc(NT) for _ in range(8)]
            ntile = 5 if has_s else 4
            gtok = ar.alloc(8 * ntile)
            btok = ar.alloc(8 * ntile)
            atmp = ar.alloc(8)
            wab = wload(w_in[l][:, O_AC:O_AC + 16], 16, 16)
            for i in range(ntile):
                n = 128 if i < 4 else NS
                c0 = i * 128
                ps = quarter[7][i % 4][0:n, 0:16]
                for k in range(16):
                    mm(ps, xTr[k][:, c0:c0 + n], wab[:, k, :], start=(k == 0), stop=(k == 15))
                tt(atmp[0:n, :], ps[:, 0:8], dtb[l][0:n, :], ALU.add)
                act(atmp[0:n, :], atmp[0:n, :], AF.Exp)
                act(atmp[0:n, :], atmp[0:n, :], AF.Ln, bias=one_c[0:n, :])
                tt(gtok[0:n, i * 8:(i + 1) * 8], atmp[0:n, :], negea[l][0:n, :], ALU.mult)
                act(btok[0:n, i * 8:(i + 1) * 8], ps[:, 8:16], AF.Sigmoid)
            xpp = [ar.alloc(NT)] * 3
            qa = [ar.alloc(NT) for _ in range(3)]
            sq = xpp[0]
            sgate = ar.alloc(NT)
            ktok = [ar.alloc(128) for _ in range(4)]
            vtk = [ar.alloc(128) for _ in range(4)]
            if has_s:
                xps = [ar.alloc(NSEQ * 7)] * 3
                cst1 = ar.alloc(128, parts=48)
                ktk_s = [x[0:4, :] for x in ktok]
                vtk_s = [x[0:4, :] for x in vtk]
                Ss = [ar.alloc(128) for _ in range(4)]
                cvst = ar.alloc(128, parts=NS)
                gts_all = ar.alloc(128, parts=4).re("p (s e) -> p s e", e=8)
                bts_all = ar.alloc(128, parts=4).re("p (s e) -> p s e", e=8)
                for sq_ in range(NSEQ):
                    pg_ = slot64[7][2 + sq_ % 2][0:4, 0:16]
                    mm(pg_[:, 0:8], ident[0:NS, 4 * sq_:4 * sq_ + 4], gtok[0:NS, 32:40])
                    mm(pg_[:, 8:16], ident[0:NS, 4 * sq_:4 * sq_ + 4], btok[0:NS, 32:40])
                    cp(gts_all[:, sq_, :], pg_[:, 0:8], eng="act")
                    cp(bts_all[:, sq_, :], pg_[:, 8:16], eng="dve")
            if blk == NBLK - 1:
                cvpt = ar.alloc(128, parts=3)
            slots = make_slots(4, 4)
            for h in range(HC):
                wq = [wload(w_in[l][:, O_QKV + j * 1024 + h * 128:O_QKV + j * 1024 + (h + 1) * 128], 16, 128) for j in range(3)]
                for j in range(3):
                    cc = j * 8 + h
                    if has_s:
                        P.dma("sp", out=cst1, in_=st_cv[l][:, j * 1024 + h * 128:j * 1024 + (h + 1) * 128])
                    for gi, (col0, n) in enumerate(groups):
                        ps = ps_for(gi, n)
                        for k in range(16):
                            mm(ps, wq[j][:, k, :], xT[k][:, col0:col0 + n].r(), start=(k == 0), stop=(k == 15))
                        cw = lambda t: convw_c[l][:, t * 24 + cc:t * 24 + cc + 1]
                        if gi == 0:
                            cp(xpp[j][:, 0:3], ctail[l][cc], eng="dve")
                            cp(xpp[j][:, 3:3 + BLK], ps, eng="act")
                            cp(ctail[l][cc], xpp[j][:, BLK:BLK + 3], eng="dve")
                            acc = qa[j][:, 0:BLK]
                            ts(acc, xpp[j][:, 0:BLK], cw(0), ALU.mult)
                            for t in range(1, 4):
                                stt(acc, xpp[j][:, t:t + BLK], cw(t), acc, ALU.mult, ALU.add)
                            act(acc, acc, AF.Silu)
                        else:
                            x3 = xps[j].re("p (s t) -> p s t", t=7)
                            pt = slot64[7][4 + j][:, 0:48]
                            tr(pt, cst1, 48)
                            cp(x3[:, :, 0:3], pt.re("p (s t) -> p s t", t=3), eng="dve")
                            cp(x3[:, :, 3:7], ps.re("p (s t) -> p s t", t=4), eng="act")
                            acc = qa[j][:, BLK:NT].re("p (s t) -> p s t", t=4)
                            ts(acc, x3[:, :, 0:4], cw(0), ALU.mult)
                            for t in range(1, 4):
                                stt(acc, x3[:, :, t:t + 4], cw(t), acc, ALU.mult, ALU.add)
                            act(qa[j][:, BLK:NT], qa[j][:, BLK:NT], AF.Silu)
                            pc_ = slot64[7][7][0:NS, :]
                        if gi == 1:
                            for half in range(2):
                                pr_ = slot64[6][half][0:NS, :]
                                for k in range(16):
                                    mm(pr_, xT[k][:, BLK:NT].r(), wq[j][:, k, half * 64:(half + 1) * 64], start=(k == 0), stop=(k == 15))
                                cp(cvst[:, half * 64:(half + 1) * 64], pr_, eng="act")
                            for t in range(1, 4):
                                P.dma("sp", is_out=True, out=cv_s[l][:, t - 1, j * 1024 + h * 128:j * 1024 + (h + 1) * 128], in_=cvst[t:NS:4, :])
                        if gi == 0 and blk == NBLK - 1:
                            for half in range(2):
                                pr_ = slot64[6][2 + half][0:3, :]
                                for k in range(16):
                                    mm(pr_, xT[k][:, BLK - 3:BLK].r(), wq[j][:, k, half * 64:(half + 1) * 64], start=(k == 0), stop=(k == 15))
                                cp(cvpt[:, half * 64:(half + 1) * 64], pr_, eng="act")
                            P.dma("sp", is_out=True, out=cv_p[l][:, j * 1024 + h * 128:j * 1024 + (h + 1) * 128], in_=cvpt)
                for j in range(2):
                    for gi, (col0, n) in enumerate(groups):
                        act(sq[:, col0:col0 + n], qa[j][:, col0:col0 + n], AF.Square)
                        ps = ps_for(gi, n)
                        mm(ps, ones, sq[:, col0:col0 + n])
                        act(sq[:, col0:col0 + n], ps, AF.Ln, bias=eps_n)
                        act(sq[:, col0:col0 + n], sq[:, col0:col0 + n], AF.Exp, scale=-0.5)
                        if j == 0:
                            stt(qa[j][:, col0:col0 + n], qa[j][:, col0:col0 + n], 128.0 ** -0.5, sq[:, col0:col0 + n], ALU.mult, ALU.mult)
                        else:
                            tt(qa[j][:, col0:col0 + n], qa[j][:, col0:col0 + n], sq[:, col0:col0 + n], ALU.mult)
                wg = wload(w_in[l][:, O_GC + h * 128:O_GC + (h + 1) * 128], 16, 128)
                for gi, (col0, n) in enumerate(groups):
                    ps = ps_for(gi, n)
                    for k in range(16):
                        mm(ps, wg[:, k, :], xT[k][:, col0:col0 + n].r(), start=(k == 0), stop=(k == 15))
                    act(sgate[:, col0:col0 + n], ps, AF.Silu)
                pk, pv = bankV[3], bankV[3]
                for sb in range(4):
                    cs = slice(sb * 128, (sb + 1) * 128)
                    tr(quarter[3][sb], qa[1][:, cs], 128)
                    cp(ktok[sb], quarter[3][sb], eng="act")
                for sb in range(4):
                    cs = slice(sb * 128, (sb + 1) * 128)
                    tr(quarter[3][sb], qa[2][:, cs], 128)
                    cp(vtk[sb], quarter[3][sb], eng="dve")
                items = []
                for sb in range(4):
                    cs = slice(sb * 128, (sb + 1) * 128)
                    items.append(dict(T=128, kT=qa[1][:, cs], qT=qa[0][:, cs], vT=qa[2][:, cs], ktok=ktok[sb], vtok=vtk[sb],
                                      gcol=gtok[:, sb * 8 + h:sb * 8 + h + 1], bcol=btok[:, sb * 8 + h:sb * 8 + h + 1], S=gS[l][h],
                                      odst=ycT[h][:, cs], sg=sgate[:, cs], ncol=normg_c[l][:, 0:1]))
                gdn_items(items, slots)
                if blk == NBLK - 1:
                    P.dma("sp", is_out=True, out=dl_p[l, h], in_=gS[l][h])
                if has_s:
                    for s0 in range(0, NSEQ, 4):
                        its = []
                        for si in range(4):
                            sq_ = s0 + si
                            c4 = slice(BLK + sq_ * 4, BLK + sq_ * 4 + 4)
                            pkq = slot64[3][si][0:4, :]
                            P.dma("sp", out=Ss[si], in_=st_dl[l, sq_, h])
                            tr(quarter[3][si][0:4, :], qa[1][:, c4], 128)
                            cp(ktk_s[si], quarter[3][si][0:4, :], eng="act")
                        for si in range(4):
                            sq_ = s0 + si
                            c4 = slice(BLK + sq_ * 4, BLK + sq_ * 4 + 4)
                            tr(quarter[3][si][0:4, :], qa[2][:, c4], 128)
                            cp(vtk_s[si], quarter[3][si][0:4, :], eng="dve")
                            r4 = slice(sq_ * 4, sq_ * 4 + 4)
                            its.append(dict(T=4, kT=qa[1][:, c4], qT=qa[0][:, c4], vT=qa[2][:, c4], ktok=ktk_s[si], vtok=vtk_s[si],
                                            gcol=gts_all[:, sq_, h:h + 1], bcol=bts_all[:, sq_, h:h + 1], S=Ss[si],
                                            odst=ycT[h][:, c4], sg=sgate[:, c4], ncol=normg_c[l][:, 0:1]))
                        gdn_items(its, slots)
                        for si in range(4):
                            P.dma("sp", is_out=True, out=dl_s[l, s0 + si, h], in_=Ss[si])
            mtmp = [sq[:, 0:BLK], sgate[:, 0:BLK]]
            merge_branch(l, 2, w_pc[l], ycT, groups)

        def phase_D(l, blk, groups):
            ar.reset()
            ar16.reset()
            last = (l == NL - 1)
            rT = [ar.alloc(NT) for _ in range(16)]
            sqt = [ar.alloc(BLK), ar.alloc(NS)]
            mean_t = [ar.alloc(BLK), ar.alloc(NS)]
            rstd_t = [ar.alloc(BLK), ar.alloc(NS)]
            psum_s = [bankV[5], slot64[7][0]]
            psum_q = [bankV[6], slot64[7][1]]
            for m in range(16):
                w = wload(w_out[l][:, m * 128:(m + 1) * 128], 16, 128)
                for gi, (col0, n) in enumerate(groups):
                    ps = ps_for(gi, n)
                    for k in range(16):
                        mm(ps, w[:, k, :], mg[k][:, col0:col0 + n].r(), start=(k == 0), stop=(k == 15))
                    rm = rT[m][:, col0:col0 + n]
                    stt(rm, xT[m][:, col0:col0 + n], ALPHA, ps, ALU.mult, ALU.add)
                    sq_ = sqt[gi][:, 0:n]
                    act(sq_, rm, AF.Square)
                    mm(psum_s[gi][:, 0:n], ones, rm, start=(m == 0), stop=(m == 15))
                    mm(psum_q[gi][:, 0:n], ones, sq_, start=(m == 0), stop=(m == 15))
            for gi, (col0, n) in enumerate(groups):
                me, rs_ = mean_t[gi][:, 0:n], rstd_t[gi][:, 0:n]
                act(me, psum_s[gi][:, 0:n], AF.Copy, scale=1.0 / D)
                act(rs_, me, AF.Square)
                stt(rs_, psum_q[gi][:, 0:n], 1.0 / D, rs_, ALU.mult, ALU.subtract)
                act(rs_, rs_, AF.Sqrt, bias=eps_ln)
                P.I("dve", "reciprocal", out=rs_, in_=rs_)
                for m in range(16):
                    rm = rT[m][:, col0:col0 + n]
                    tt(rm, rm, me, ALU.subtract)
                    tt(rm, rm, rs_, ALU.mult)
                    if last:
                        ts(rm, rm, lng_c[l][:, m:m + 1], ALU.mult, lnb_c[l][:, m:m + 1], ALU.add)
                    else:
                        ts(xT[m][:, col0:col0 + n].r(), rm, lng_c[l][:, m:m + 1], ALU.mult, lnb_c[l][:, m:m + 1], ALU.add)
            if last:
                ystg = [ar.alloc(BLK), ar.alloc(BLK)]
                tiles = [(y_p[blk * BLK + i * 128: blk * BLK + (i + 1) * 128, :], i * 128, 128) for i in range(4)]
                if len(groups) > 1:
                    tiles.append((y_s, BLK, NS))
                ci = 0
                for ti, (dst, col0, n) in enumerate(tiles):
                    for k4 in range(4):
                        s_ = ystg[ci % 2]
                        ci += 1
                        ps = ps_main()
                        for j in range(4):
                            k = k4 * 4 + j
                            tr(ps[0:n, j * 128:(j + 1) * 128], rT[k][:, col0:col0 + n], 128)
                        cp(s_[0:n, :], ps[0:n, :], eng=("act" if k4 % 2 == 0 else "dve"))
                        P.dma("sp", is_out=True, out=dst[:, k4 * 512:(k4 + 1) * 512], in_=s_[0:n, :])

        krow512 = misc_alloc(BLK)
        halfpi = misc_alloc(1)
        P.I("pool", "iota", out=krow512, pattern=[[1, BLK]], base=1, channel_multiplier=0, allow_small_or_imprecise_dtypes=True)
        P.I("pool", "memset", ap=halfpi, constant=math.pi / 2)
        eps_ln = misc_alloc(1)
        eps_n = misc_alloc(1)
        one_c = misc_alloc(1)
        P.I("pool", "memset", ap=eps_ln, constant=1e-5)
        P.I("pool", "memset", ap=eps_n, constant=1e-6)
        P.I("pool", "memset", ap=one_c, constant=1.0)
        ws_cache = []

        import os
        kstop = int(os.environ.get("KSTOP", "100000"))
        nph = [0]

        def go(fn, *a):
            if nph[0] < kstop:
                fn(*a)
            nph[0] += 1

        for blk in range(NBLK):
            groups = [(0, BLK)] + ([(BLK, NS)] if blk == 0 else [])
            go(load_x, blk, groups)
            for l in range(NL):
                go(phase_A, l, blk, groups)
                go(phase_B, l, blk, groups)
                go(phase_C, l, blk, groups)
                go(phase_D, l, blk, groups)
        P.emit()
    return nc


_CACHE = {}


def kernel(**inp):
    f = lambda a: np.ascontiguousarray(np.asarray(a, dtype=np.float32))
    if "nc" not in _CACHE:
        _CACHE["nc"] = build_program()
    nc = _CACHE["nc"]
    shared = dict(
        w_in=f(inp["w_in"]), b_gate=f(inp["b_gate"]).reshape(NL, 48, 128),
        lam_re=f(inp["lam_re"]).reshape(NL, 32, 128), lam_im=f(inp["lam_im"]).reshape(NL, 32, 128),
        log_dt=f(inp["log_dt"]).reshape(NL, 32, 2),
        b_re=f(inp["ssm_b_re"]).reshape(NL, 4096, 16), b_im=f(inp["ssm_b_im"]).reshape(NL, 4096, 16),
        c_re=f(inp["ssm_c_re"]).reshape(NL, 1024, 64), c_im=f(inp["ssm_c_im"]).reshape(NL, 1024, 64),
        ssm_d=f(inp["ssm_d"]).reshape(NL, 8, 128), w_glu=f(inp["w_glu"]), b_glu=f(inp["b_glu"]).reshape(NL, 8, 128),
        w_pa=f(inp["w_pa"]), ln_v_g=f(inp["ln_v_g"]), ln_v_b=f(inp["ln_v_b"]), w_s=f(inp["w_s"]),
        b_s=f(inp["b_s"]).reshape(NL, 1024), w_pb=f(inp["w_pb"]), conv_w=f(inp["conv_w"]).reshape(NL, 96, 128),
        a_log=f(inp["a_log"]), dt_bias=f(inp["dt_bias"]), norm_g=f(inp["gdn_norm_g"]).reshape(NL, 1, 128),
        w_pc=f(inp["w_pc"]), w_out=f(inp["w_out"]), ln_g=f(inp["ln_g"]).reshape(NL, 16, 128),
        ln_b=f(inp["ln_b"]).reshape(NL, 16, 128),
    )
    xpr = f(inp["x_prompt"])
    xsm = f(inp["x_sample"])
    sre = f(inp["state_ssm_re"])
    sim = f(inp["state_ssm_im"])
    sdl = f(inp["state_delta"])
    scv = f(inp["state_conv"])
    in_maps = []
    for c in range(8):
        sl = slice(c * NSEQ, (c + 1) * NSEQ)
        m = dict(shared)
        m["xp"] = xpr[c % 4]
        m["xs"] = np.ascontiguousarray(xsm[sl].reshape(NS, D))
        m["st_re"] = np.ascontiguousarray(sre[:, sl].reshape(NL, NSEQ, 4096))
        m["st_im"] = np.ascontiguousarray(sim[:, sl].reshape(NL, NSEQ, 4096))
        m["st_dl"] = np.ascontiguousarray(sdl[:, sl])
        m["st_cv"] = np.ascontiguousarray(scv[:, sl].reshape(NL, NSEQ * 3, QKV))
        in_maps.append(m)
    import os
    ncores = int(os.environ.get("KCORES", "8"))
    res = run_bass_kernel_spmd(nc, in_maps[:ncores], core_ids=list(range(ncores))).results
    if ncores < 8:
        res = list(res) + [res[0]] * (8 - ncores)
    y_prompt = np.stack([res[c]["y_p"] for c in range(4)])
    y_sample = np.concatenate([res[c]["y_s"].reshape(NSEQ, 4, D) for c in range(8)], 0)
    ssm_re_p = np.stack([res[c]["sre_p"].reshape(NL, 64, 64) for c in range(4)], 1)
    ssm_im_p = np.stack([res[c]["sim_p"].reshape(NL, 64, 64) for c in range(4)], 1)
    delta_p = np.stack([res[c]["dl_p"] for c in range(4)], 1)
    conv_p = np.stack([res[c]["cv_p"] for c in range(4)], 1)
    ssm_re_s = np.concatenate([res[c]["sre_s"].reshape(NL, NSEQ, 64, 64) for c in range(8)], 1)
    ssm_im_s = np.concatenate([res[c]["sim_s"].reshape(NL, NSEQ, 64, 64) for c in range(8)], 1)
    delta_s = np.concatenate([res[c]["dl_s"] for c in range(8)], 1)
    conv_s = np.concatenate([res[c]["cv_s"] for c in range(8)], 1)
    gv = np.concatenate([res[c]["gv_s"].reshape(NL, NSEQ, 4, 1024) for c in range(8)], 1)
    return tuple(np.ascontiguousarray(a, dtype=np.float32) for a in
                 (y_prompt, y_sample, ssm_re_p, ssm_im_p, delta_p, conv_p, ssm_re_s, ssm_im_s, delta_s, conv_s, gv))
```
